# Optimizing a Trainium2 kernel written in Bass

```python
import functools
import jax, jax.numpy as jnp
from jax import lax
import numpy as np


D_MODEL = 2048
BATCH = 1
SEQ = 8192
DEPTH = 4

N_EVEN = (DEPTH + 1) // 2
N_ODD = DEPTH // 2

D_FF = 4096

GDN_HEADS = 8
GDN_DK = 128
GDN_DV = 128
GDN_QK_W = GDN_HEADS * GDN_DK
GDN_V_W = GDN_HEADS * GDN_DV
CONV_K = 4
CHUNK = 64
POOL_WINDOWS = (2, 4, 8, 16)
POOL_GROUPS = 4
POOL_W = D_MODEL // 2
POOL_GROUP_W = POOL_W // POOL_GROUPS
EVEN_IN = 2 * GDN_QK_W + 2 * GDN_V_W + 2 * GDN_HEADS + POOL_W
EVEN_MIX = GDN_V_W + POOL_W

MLA_HEADS = 16
Q_LORA = 512
KV_LORA = 512
NOPE = 128
ROPE = 64
V_HEAD = 128
QK_HEAD = NOPE + ROPE
ODD_IN = Q_LORA + KV_LORA + ROPE
ROPE_THETA = 10000.0
Q_BLOCK = 128

EPS = 1e-6

kernel_name = 'hybrid_gdn_pool_mla_macaron'


def rms_norm(x, gain):
    xf = x.astype(jnp.float32)
    y = xf * lax.rsqrt(jnp.mean(xf * xf, axis=-1, keepdims=True) + EPS)
    return (y * gain.astype(jnp.float32)).astype(x.dtype)


def l2_norm(x):
    xf = x.astype(jnp.float32)
    return xf * lax.rsqrt(jnp.sum(xf * xf, axis=-1, keepdims=True) + EPS)


def swiglu(h, w_gate, w_up, w_down):
    return (jax.nn.silu(h @ w_gate) * (h @ w_up)) @ w_down


def causal_dwconv(x, w):
    c = x.shape[-1]
    return lax.conv_general_dilated(x, w[:, None, :].astype(x.dtype), window_strides=(1,),
                                    padding=[(CONV_K - 1, 0)],
                                    dimension_numbers=('NWC', 'WIO', 'NWC'),
                                    feature_group_count=c)


def gated_delta_rule(q, k, v, g, beta):
    B, S, H, DK = q.shape
    N = S // CHUNK

    def chunks(t):
        return t.reshape(B, N, CHUNK, H, -1).transpose(0, 3, 1, 2, 4)

    q = chunks(q) * DK ** -0.5
    k = chunks(k)
    v = chunks(v)
    g = chunks(g[..., None])[..., 0]
    beta = chunks(beta[..., None])[..., 0]
    gc = jnp.cumsum(g, axis=-1)
    idx = jnp.arange(CHUNK)
    causal = idx[:, None] >= idx[None, :]
    strict = idx[:, None] > idx[None, :]
    decay = jnp.exp(jnp.where(causal, gc[..., :, None] - gc[..., None, :], -jnp.inf))
    kb = k * beta[..., None]
    vb = v * beta[..., None]
    m = jnp.where(strict, jnp.einsum('bhnid,bhnjd->bhnij', kb, k) * decay, 0.0)
    tri = m + jnp.eye(CHUNK, dtype=m.dtype)
    solve = functools.partial(lax.linalg.triangular_solve, left_side=True, lower=True,
                              unit_diagonal=True)
    u = solve(tri, vb)
    w = solve(tri, kb * jnp.exp(gc)[..., None])
    attn = jnp.einsum('bhnid,bhnjd->bhnij', q, k) * decay
    g_last = gc[..., -1]
    q_dec = q * jnp.exp(gc)[..., None]
    k_dec = k * jnp.exp(g_last[..., None] - gc)[..., None]

    def step(state, xs):
        qd, kd, uc, wc, ac, gl = xs
        v_new = uc - jnp.einsum('bhck,bhkv->bhcv', wc, state)
        out = jnp.einsum('bhck,bhkv->bhcv', qd, state) + jnp.einsum('bhij,bhjv->bhiv', ac, v_new)
        state = state * jnp.exp(gl)[..., None, None] + jnp.einsum('bhck,bhcv->bhkv', kd, v_new)
        return state, out

    xs = (jnp.moveaxis(q_dec, 2, 0), jnp.moveaxis(k_dec, 2, 0), jnp.moveaxis(u, 2, 0),
          jnp.moveaxis(w, 2, 0), jnp.moveaxis(attn, 2, 0), jnp.moveaxis(g_last, 2, 0))
    state0 = jnp.zeros((B, H, DK, v.shape[-1]), jnp.float32)
    _, o = lax.scan(step, state0, xs)
    return o.transpose(1, 0, 3, 2, 4).reshape(B, S, H, -1)


def multiscale_pool(u, pool_w, pool_scale):
    B, S, C = u.shape
    ug = u.astype(jnp.float32).reshape(B, S, POOL_GROUPS, POOL_GROUP_W)
    cs = jnp.cumsum(ug, axis=1)
    t = jnp.arange(S)
    pooled = []
    for gi, win in enumerate(POOL_WINDOWS):
        c = cs[:, :, gi]
        prev = jnp.pad(c, ((0, 0), (win, 0), (0, 0)))[:, :S]
        cnt = jnp.minimum(t + 1, win).astype(jnp.float32)[None, :, None]
        pooled.append((c - prev) / cnt)
    diff = (jnp.stack(pooled, axis=2) - ug).astype(u.dtype)
    y = jnp.einsum('bsgc,gcd->bsgd', diff, pool_w)
    return y.reshape(B, S, C) * pool_scale


def delta_pool_mixer(h, w_in, conv_w, a_log, dt_bias, out_norm, pool_w, pool_scale, w_out):
    B, S, _ = h.shape
    proj = h @ w_in
    o1 = 2 * GDN_QK_W + GDN_V_W
    o2 = o1 + GDN_V_W
    o3 = o2 + GDN_HEADS
    o4 = o3 + GDN_HEADS
    qkv, z, a, b, u = jnp.split(proj, [o1, o2, o3, o4], axis=-1)
    qkv = jax.nn.silu(causal_dwconv(qkv, conv_w))
    q, k, v = jnp.split(qkv, [GDN_QK_W, 2 * GDN_QK_W], axis=-1)
    q = l2_norm(q.reshape(B, S, GDN_HEADS, GDN_DK))
    k = l2_norm(k.reshape(B, S, GDN_HEADS, GDN_DK))
    v = v.reshape(B, S, GDN_HEADS, GDN_DV).astype(jnp.float32)
    beta = jax.nn.sigmoid(b.astype(jnp.float32))
    g = -jnp.exp(a_log.astype(jnp.float32)) * jax.nn.softplus(a.astype(jnp.float32) + dt_bias.astype(jnp.float32))
    o = gated_delta_rule(q, k, v, g, beta)
    zf = z.reshape(B, S, GDN_HEADS, GDN_DV).astype(jnp.float32)
    o = (rms_norm(o, out_norm) * jax.nn.silu(zf)).reshape(B, S, GDN_V_W).astype(h.dtype)
    p = multiscale_pool(u, pool_w, pool_scale)
    return jnp.concatenate([o, p], axis=-1) @ w_out


def seg_head_norm(t, gain):
    tf = t.astype(jnp.float32)
    nope, pe = tf[..., :NOPE], tf[..., NOPE:]
    nope = nope * lax.rsqrt(jnp.mean(nope * nope, axis=-1, keepdims=True) + EPS)
    pe = pe * lax.rsqrt(jnp.mean(pe * pe, axis=-1, keepdims=True) + EPS)
    return (jnp.concatenate([nope, pe], axis=-1) * gain.astype(jnp.float32)).astype(t.dtype)


def apply_rope_tail(t, cos, sin):
    nope, pe = t[..., :NOPE], t[..., NOPE:].astype(jnp.float32)
    x1, x2 = pe[..., :ROPE // 2], pe[..., ROPE // 2:]
    rot = jnp.concatenate([x1 * cos - x2 * sin, x2 * cos + x1 * sin], axis=-1)
    return jnp.concatenate([nope, rot.astype(t.dtype)], axis=-1)


def causal_block_attention(q, k, v):
    B, S, H, Dqk = q.shape
    nb = S // Q_BLOCK
    qb = q.reshape(B, nb, Q_BLOCK, H, Dqk).transpose(1, 0, 2, 3, 4)
    starts = jnp.arange(nb, dtype=jnp.int32) * Q_BLOCK
    kpos = jnp.arange(S, dtype=jnp.int32)
    scale = Dqk ** -0.5
    neg = jnp.finfo(jnp.float32).min

    def one_block(args):
        qi, s0 = args
        s = jnp.einsum('bqhd,bkhd->bhqk', qi, k, preferred_element_type=jnp.float32) * scale
        qpos = s0 + jnp.arange(Q_BLOCK, dtype=jnp.int32)
        s = jnp.where(qpos[:, None] >= kpos[None, :], s, neg)
        p = jax.nn.softmax(s, axis=-1)
        return jnp.einsum('bhqk,bkhd->bqhd', p.astype(v.dtype), v)

    o = lax.map(one_block, (qb, starts))
    return o.transpose(1, 0, 2, 3, 4).reshape(B, S, H, v.shape[-1])


def mla_mixer(h, positions, w_in, q_norm, kv_norm, w_q_up, w_kv_up, q_head_norm, k_head_norm, w_out):
    B, S, _ = h.shape
    proj = h @ w_in
    q_lat, kv_lat, k_pe = jnp.split(proj, [Q_LORA, Q_LORA + KV_LORA], axis=-1)
    q = (rms_norm(q_lat, q_norm) @ w_q_up).reshape(B, S, MLA_HEADS, QK_HEAD)
    kv = (rms_norm(kv_lat, kv_norm) @ w_kv_up).reshape(B, S, MLA_HEADS, NOPE + V_HEAD)
    k_nope, v = kv[..., :NOPE], kv[..., NOPE:]
    k_pe = jnp.broadcast_to(k_pe[:, :, None, :], (B, S, MLA_HEADS, ROPE))
    k = jnp.concatenate([k_nope, k_pe], axis=-1)
    q = seg_head_norm(q, q_head_norm)
    k = seg_head_norm(k, k_head_norm)
    inv_freq = ROPE_THETA ** (-jnp.arange(0, ROPE, 2, dtype=jnp.float32) / ROPE)
    ang = positions.astype(jnp.float32)[..., None] * inv_freq
    cos, sin = jnp.cos(ang)[:, :, None, :], jnp.sin(ang)[:, :, None, :]
    q = apply_rope_tail(q, cos, sin)
    k = apply_rope_tail(k, cos, sin)
    o = causal_block_attention(q, k, v)
    return o.reshape(B, S, MLA_HEADS * V_HEAD) @ w_out


def setup_inputs(seed: int = 0) -> dict:
    key = jax.random.key(seed)
    ks = jax.random.split(key, 32)

    def dense(k, shape, fan_in):
        return jax.random.normal(k, shape, jnp.float32) * fan_in ** -0.5

    def gain(k, shape):
        return 1.0 + 0.02 * jax.random.normal(k, shape, jnp.float32)

    x = jax.random.normal(ks[0], (BATCH, SEQ, D_MODEL), jnp.float32)
    positions = jnp.broadcast_to(jnp.arange(SEQ, dtype=jnp.int32)[None, :], (BATCH, SEQ))
    dt = jnp.exp(jax.random.uniform(ks[14], (N_EVEN, GDN_HEADS), jnp.float32,
                                    np.log(1e-3), np.log(1e-1)))
    return {
        'x': x,
        'positions': positions,
        'ffn1_norm': gain(ks[1], (DEPTH, D_MODEL)),
        'ffn1_w_gate': dense(ks[2], (DEPTH, D_MODEL, D_FF), D_MODEL),
        'ffn1_w_up': dense(ks[3], (DEPTH, D_MODEL, D_FF), D_MODEL),
        'ffn1_w_down': dense(ks[4], (DEPTH, D_FF, D_MODEL), D_FF),
        'mix_norm': gain(ks[5], (DEPTH, D_MODEL)),
        'ffn2_norm': gain(ks[6], (DEPTH, D_MODEL)),
        'ffn2_w_gate': dense(ks[7], (DEPTH, D_MODEL, D_FF), D_MODEL),
        'ffn2_w_up': dense(ks[8], (DEPTH, D_MODEL, D_FF), D_MODEL),
        'ffn2_w_down': dense(ks[9], (DEPTH, D_FF, D_MODEL), D_FF),
        'hyb_w_in': dense(ks[10], (N_EVEN, D_MODEL, EVEN_IN), D_MODEL),
        'gdn_conv': dense(ks[11], (N_EVEN, CONV_K, 2 * GDN_QK_W + GDN_V_W), CONV_K),
        'gdn_a_log': jnp.log(jax.random.uniform(ks[12], (N_EVEN, GDN_HEADS), jnp.float32, 1.0, 16.0)),
        'gdn_dt_bias': dt + jnp.log(-jnp.expm1(-dt)),
        'gdn_out_norm': gain(ks[13], (N_EVEN, GDN_DV)),
        'pool_w': dense(ks[15], (N_EVEN, POOL_GROUPS, POOL_GROUP_W, POOL_GROUP_W), POOL_GROUP_W),
        'pool_scale': gain(ks[16], (N_EVEN, POOL_W)),
        'hyb_w_out': dense(ks[17], (N_EVEN, EVEN_MIX, D_MODEL), EVEN_MIX),
        'mla_w_in': dense(ks[18], (N_ODD, D_MODEL, ODD_IN), D_MODEL),
        'mla_q_norm': gain(ks[19], (N_ODD, Q_LORA)),
        'mla_kv_norm': gain(ks[20], (N_ODD, KV_LORA)),
        'mla_w_q_up': dense(ks[21], (N_ODD, Q_LORA, MLA_HEADS * QK_HEAD), Q_LORA),
        'mla_w_kv_up': dense(ks[22], (N_ODD, KV_LORA, MLA_HEADS * (NOPE + V_HEAD)), KV_LORA),
        'mla_q_head_norm': gain(ks[23], (N_ODD, QK_HEAD)),
        'mla_k_head_norm': gain(ks[24], (N_ODD, QK_HEAD)),
        'mla_w_out': dense(ks[25], (N_ODD, MLA_HEADS * V_HEAD, D_MODEL), MLA_HEADS * V_HEAD),
    }


def reference(x, positions, ffn1_norm, ffn1_w_gate, ffn1_w_up, ffn1_w_down, mix_norm,
              ffn2_norm, ffn2_w_gate, ffn2_w_up, ffn2_w_down,
              hyb_w_in, gdn_conv, gdn_a_log, gdn_dt_bias, gdn_out_norm, pool_w, pool_scale, hyb_w_out,
              mla_w_in, mla_q_norm, mla_kv_norm, mla_w_q_up, mla_w_kv_up,
              mla_q_head_norm, mla_k_head_norm, mla_w_out):
    for layer in range(DEPTH):
        x = x + 0.5 * swiglu(rms_norm(x, ffn1_norm[layer]), ffn1_w_gate[layer], ffn1_w_up[layer], ffn1_w_down[layer])
        h = rms_norm(x, mix_norm[layer])
        i = layer // 2
        if layer % 2 == 0:
            x = x + delta_pool_mixer(h, hyb_w_in[i], gdn_conv[i], gdn_a_log[i], gdn_dt_bias[i],
                                     gdn_out_norm[i], pool_w[i], pool_scale[i], hyb_w_out[i])
        else:
            x = x + mla_mixer(h, positions, mla_w_in[i], mla_q_norm[i], mla_kv_norm[i], mla_w_q_up[i],
                              mla_w_kv_up[i], mla_q_head_norm[i], mla_k_head_norm[i], mla_w_out[i])
        x = x + 0.5 * swiglu(rms_norm(x, ffn2_norm[layer]), ffn2_w_gate[layer], ffn2_w_up[layer], ffn2_w_down[layer])
    return x
```

```python
from contextlib import ExitStack
import numpy as np
import ml_dtypes
import concourse.bass as bass
import concourse.mybir as mybir
from concourse.bass_utils import run_bass_kernel_spmd

F32 = mybir.dt.float32
BF16 = mybir.dt.bfloat16
AF = mybir.ActivationFunctionType
ALU = mybir.AluOpType
AX = mybir.AxisListType

NCORES = 8
D = 2048
DFF = 4096
EPS = 1e-6


class Prog:
    ENG = ("tensor", "vector", "scalar", "gpsimd", "sync")

    def __init__(self, nc):
        self.nc = nc
        self.ops = {e: [] for e in self.ENG}
        self.cnt = {}
        self.last_w = {}
        self.readers = {}
        self.seen = {e: {} for e in self.ENG}

    def add(self, eng, fn, reads=(), writes=(), dma=None):
        deps = {}

        def need(sig):
            s, v = sig
            if deps.get(s, 0) < v:
                deps[s] = v

        for r in reads:
            if r in self.last_w:
                need(self.last_w[r])
        for w in writes:
            if w in self.last_w:
                need(self.last_w[w])
            for s, v in self.readers.get(w, {}).items():
                need((s, v))
        waits = []
        for s, v in deps.items():
            if eng == "tensor" and s == "E:tensor":
                continue
            if self.seen[eng].get(s, 0) >= v:
                continue
            self.seen[eng][s] = v
            waits.append((s, v))
        if fn is None:
            self.ops[eng].append((None, waits, None))
            return
        if dma is not None:
            s = "D:" + dma
            self.cnt[s] = self.cnt.get(s, 0) + 16
            sig = (s, self.cnt[s])
            inc = (s, 16)
        else:
            s = "E:" + eng
            self.cnt[s] = self.cnt.get(s, 0) + 1
            sig = (s, self.cnt[s])
            inc = (s, 1)
        for w in writes:
            self.last_w[w] = sig
            self.readers[w] = {}
        for r in reads:
            d = self.readers.setdefault(r, {})
            if d.get(sig[0], 0) < sig[1]:
                d[sig[0]] = sig[1]
        self.ops[eng].append((fn, waits, inc))

    def barrier(self):
        for eng in self.ENG:
            waits = []
            for s, v in self.cnt.items():
                if eng == "tensor" and s == "E:tensor":
                    continue
                if self.seen[eng].get(s, 0) >= v:
                    continue
                self.seen[eng][s] = v
                waits.append((s, v))
            self.ops[eng].append((None, waits, None))

    def finish(self, out_res):
        self.add("sync", None, reads=list(out_res))

    def emit(self):
        nc = self.nc
        with ExitStack() as es:
            sems = {}
            for i, s in enumerate(sorted(self.cnt)):
                sems[s] = es.enter_context(nc.semaphore("s%d" % i))
            block = es.enter_context(nc.Block())

            def runner(ename):
                def run(eng):
                    for fn, waits, inc in self.ops[ename]:
                        for s, v in waits:
                            eng.wait_ge(sems[s], v)
                        if fn is not None:
                            ins = fn(eng)
                            ins.then_inc(sems[inc[0]], inc[1])
                return run

            block.tensor(runner("tensor"))
            block.vector(runner("vector"))
            block.scalar(runner("scalar"))
            block.gpsimd(runner("gpsimd"))
            block.sync(runner("sync"))


class Ctx:
    def __init__(self, nc, es):
        self.nc = nc
        self.es = es
        self.n = 0

    def sb(self, shape, dt, name=None):
        self.n += 1
        return self.es.enter_context(self.nc.sbuf_tensor(name or ("t%d" % self.n), list(shape), dt))

    def ps(self, shape, dt, name=None):
        self.n += 1
        return self.es.enter_context(self.nc.psum_tensor(name or ("p%d" % self.n), list(shape), dt))


def build_chain(tok, has_mix, n_ffn, has_h):
    nc = bass.Bass("TRN2", target_bir_lowering=False)
    NB = tok // 512
    DC = D // 128
    FC = DFF // 128
    xT_in = nc.dram_tensor("xT", [D, tok], F32, kind="ExternalInput").ap()
    xT_out = nc.dram_tensor("xT_out", [D, tok], F32, kind="ExternalOutput").ap()
    if has_mix:
        mixT = nc.dram_tensor("mixT", [D, tok], BF16, kind="ExternalInput").ap()
        w_out = nc.dram_tensor("w_out", [D, D], F32, kind="ExternalInput").ap()
    ffn_w = []
    for i in range(n_ffn):
        ffn_w.append(dict(
            g=nc.dram_tensor("f%d_norm" % i, [128, DC], F32, kind="ExternalInput").ap(),
            wg=nc.dram_tensor("f%d_wg" % i, [D, DFF], F32, kind="ExternalInput").ap(),
            wu=nc.dram_tensor("f%d_wu" % i, [D, DFF], F32, kind="ExternalInput").ap(),
            wd=nc.dram_tensor("f%d_wd" % i, [DFF, D], F32, kind="ExternalInput").ap(),
        ))
    if has_h:
        h_norm = nc.dram_tensor("h_norm", [128, DC], F32, kind="ExternalInput").ap()
        hT_out = nc.dram_tensor("hT_out", [D, tok], BF16, kind="ExternalOutput").ap()

    with ExitStack() as es:
        cx = Ctx(nc, es)
        P = Prog(nc)
        xb = cx.sb([128, DC, 512], F32, "xb")
        hT = cx.sb([128, DC, 512], BF16, "hT")
        aT = cx.sb([128, FC, 512], BF16, "aT")
        NW = 2
        wbuf = [cx.sb([128, 16 * 1024], BF16, "wbuf%d" % i) for i in range(NW)]
        sq = [cx.sb([128, 512], BF16, "sq%d" % i) for i in range(2)]
        rstd = cx.sb([128, 512], F32, "rstd")
        sg = [cx.sb([128, 512], F32, "sg%d" % i) for i in range(2)]
        ones = cx.sb([128, 128], BF16, "ones")
        gains = cx.sb([128, (n_ffn + 1) * DC], F32, "gains")
        NPS = 6
        ps = [cx.ps([128, 512], F32, "ps%d" % i) for i in range(NPS)]
        st = dict(ps=0, w=0, sq=0, sg=0)

        def next_ps():
            i = st["ps"] % NPS
            st["ps"] += 1
            return i

        P.add("gpsimd", lambda e: e.memset(ones[:], 1.0), writes=["ones"])
        for i in range(n_ffn):
            P.add("sync", lambda e, i=i: e.dma_start(out=gains[:, i * DC:(i + 1) * DC], in_=ffn_w[i]["g"]),
                  writes=["gains%d" % i], dma="gains%d" % i)
        if has_h:
            P.add("sync", lambda e: e.dma_start(out=gains[:, n_ffn * DC:(n_ffn + 1) * DC], in_=h_norm),
                  writes=["gains%d" % n_ffn], dma="gains%d" % n_ffn)

        def load_w(src_ap, nk, c0, ncols):
            slot = st["w"] % NW
            st["w"] += 1
            view = wbuf[slot][:, 0:nk * ncols].rearrange("p (k c) -> p k c", k=nk)
            src = src_ap.rearrange("(k p) c -> p k c", p=128)[:, :, c0:c0 + ncols]
            res = "wbuf%d" % slot
            half = nk // 2
            P.add("gpsimd", lambda e: e.dma_start(out=view[:, 0:half, :], in_=src[:, 0:half, :]),
                  writes=[res + "a"], dma=res + "a")
            P.add("gpsimd", lambda e: e.dma_start(out=view[:, half:nk, :], in_=src[:, half:nk, :]),
                  writes=[res + "b"], dma=res + "b")
            return view, [res + "a", res + "b"], half

        def norm_to_h(gi):
            pi = next_ps()
            for dc in range(DC):
                s = st["sq"] % 2
                st["sq"] += 1
                P.add("scalar", lambda e, dc=dc, s=s: e.activation(out=sq[s][:], in_=xb[:, dc, :], func=AF.Square),
                      reads=["xb"], writes=["sq%d" % s])
                P.add("tensor", lambda e, dc=dc, s=s, pi=pi: e.matmul(ps[pi][:], ones[:], sq[s][:], start=(dc == 0), stop=(dc == DC - 1)),
                      reads=["ones", "sq%d" % s], writes=["ps%d" % pi])
            P.add("scalar", lambda e, pi=pi: e.activation(out=rstd[:], in_=ps[pi][:], func=AF.Sqrt, bias=EPS, scale=1.0 / D),
                  reads=["ps%d" % pi], writes=["rstd"])
            P.add("vector", lambda e: e.reciprocal(out=rstd[:], in_=rstd[:]),
                  reads=["rstd"], writes=["rstd"])
            for dc in range(DC):
                P.add("vector", lambda e, dc=dc: e.scalar_tensor_tensor(
                    out=hT[:, dc, :], in0=xb[:, dc, :], scalar=gains[:, gi * DC + dc:gi * DC + dc + 1], in1=rstd[:],
                    op0=ALU.mult, op1=ALU.mult),
                    reads=["xb", "rstd", "gains%d" % gi], writes=["hT"])

        def proj_residual(src, src_res, w_ap, nk, scale):
            for dg in range(4):
                wv, wres, half = load_w(w_ap, nk, dg * 512, 512)
                for j in range(4):
                    dc = dg * 4 + j
                    pi = next_ps()
                    for k in range(nk):
                        P.add("tensor", lambda e, k=k, j=j, pi=pi, wv=wv: e.matmul(
                            ps[pi][:], wv[:, k, j * 128:(j + 1) * 128], src[:, k, :], start=(k == 0), stop=(k == nk - 1)),
                            reads=[wres[0] if k < half else wres[1], src_res], writes=["ps%d" % pi])
                    P.add("vector", lambda e, dc=dc, pi=pi: e.scalar_tensor_tensor(
                        out=xb[:, dc, :], in0=ps[pi][:], scalar=float(scale), in1=xb[:, dc, :], op0=ALU.mult, op1=ALU.add),
                        reads=["ps%d" % pi, "xb"], writes=["xb"])

        def ffn(i):
            norm_to_h(i)
            fw = ffn_w[i]
            for fg in range(DFF // 512):
                wgv, wgres, half = load_w(fw["wg"], DC, fg * 512, 512)
                wuv, wures, _ = load_w(fw["wu"], DC, fg * 512, 512)
                for j in range(4):
                    fc = fg * 4 + j
                    pg = next_ps()
                    for k in range(DC):
                        P.add("tensor", lambda e, k=k, j=j, pg=pg, wgv=wgv: e.matmul(
                            ps[pg][:], wgv[:, k, j * 128:(j + 1) * 128], hT[:, k, :], start=(k == 0), stop=(k == DC - 1)),
                            reads=[wgres[0] if k < half else wgres[1], "hT"], writes=["ps%d" % pg])
                    pu = next_ps()
                    for k in range(DC):
                        P.add("tensor", lambda e, k=k, j=j, pu=pu, wuv=wuv: e.matmul(
                            ps[pu][:], wuv[:, k, j * 128:(j + 1) * 128], hT[:, k, :], start=(k == 0), stop=(k == DC - 1)),
                            reads=[wures[0] if k < half else wures[1], "hT"], writes=["ps%d" % pu])
                    s = st["sg"] % 2
                    st["sg"] += 1
                    P.add("scalar", lambda e, pg=pg, s=s: e.activation(out=sg[s][:], in_=ps[pg][:], func=AF.Silu),
                          reads=["ps%d" % pg], writes=["sg%d" % s])
                    P.add("vector", lambda e, pu=pu, s=s, fc=fc: e.tensor_tensor(out=aT[:, fc, :], in0=ps[pu][:], in1=sg[s][:], op=ALU.mult),
                          reads=["ps%d" % pu, "sg%d" % s], writes=["aT"])
            proj_residual(aT, "aT", fw["wd"], FC, 0.5)

        for b in range(NB):
            tsl = slice(b * 512, (b + 1) * 512)
            P.add("sync", lambda e, tsl=tsl: e.dma_start(out=xb[:], in_=xT_in.rearrange("(c p) t -> p c t", p=128)[:, :, tsl]),
                  writes=["xb"], dma="xb")
            if has_mix:
                mv = aT[:, 0:DC, :]
                P.add("sync", lambda e, tsl=tsl, mv=mv: e.dma_start(out=mv, in_=mixT.rearrange("(c p) t -> p c t", p=128)[:, :, tsl]),
                      writes=["aT"], dma="aT")
                proj_residual(mv, "aT", w_out, DC, 1.0)
            for i in range(n_ffn):
                ffn(i)
            P.add("sync", lambda e, tsl=tsl: e.dma_start(out=xT_out.rearrange("(c p) t -> p c t", p=128)[:, :, tsl], in_=xb[:]),
                  reads=["xb"], writes=["dram_x%d" % b], dma="xout")
            if has_h:
                norm_to_h(n_ffn)
                P.add("sync", lambda e, tsl=tsl: e.dma_start(out=hT_out.rearrange("(c p) t -> p c t", p=128)[:, :, tsl], in_=hT[:]),
                      reads=["hT"], writes=["dram_h%d" % b], dma="hout")
        outs = ["dram_x%d" % b for b in range(NB)] + (["dram_h%d" % b for b in range(NB)] if has_h else [])
        P.finish(outs)
        P.emit()
    return nc


def gain_cols(g):
    return np.ascontiguousarray(np.asarray(g, np.float32).reshape(D // 128, 128).T)


_NC_CACHE = {}


def run_chain(xT, mixT, w_out, ffns, h_norm):
    S = xT.shape[1]
    tok = S // NCORES
    key = ("chain", tok, mixT is not None, len(ffns), h_norm is not None)
    if key not in _NC_CACHE:
        _NC_CACHE[key] = build_chain(tok, mixT is not None, len(ffns), h_norm is not None)
    nc = _NC_CACHE[key]
    in_maps = []
    for c in range(NCORES):
        sl = slice(c * tok, (c + 1) * tok)
        m = {"xT": np.ascontiguousarray(xT[:, sl])}
        if mixT is not None:
            m["mixT"] = np.ascontiguousarray(mixT[:, sl])
            m["w_out"] = w_out
        for i, f in enumerate(ffns):
            m["f%d_norm" % i] = gain_cols(f["norm"])
            m["f%d_wg" % i] = f["wg"]
            m["f%d_wu" % i] = f["wu"]
            m["f%d_wd" % i] = f["wd"]
        if h_norm is not None:
            m["h_norm"] = gain_cols(h_norm)
        in_maps.append(m)
    res = run_bass_kernel_spmd(nc, in_maps, core_ids=list(range(NCORES)))
    xo = np.concatenate([r["xT_out"] for r in res.results], axis=1)
    ho = np.concatenate([r["hT_out"] for r in res.results], axis=1) if h_norm is not None else None
    return xo, ho


def emit_sin(P, cx_tiles, x_ap, x_res, out_ap, out_res, shift, tag):
    import math
    ki, kf, y, m = cx_tiles
    TWO_PI = 2.0 * math.pi
    C1 = 6.28125
    C2 = TWO_PI - C1
    r = [tag + "_ki", tag + "_kf", tag + "_y", tag + "_m"]
    P.add("vector", lambda e: e.tensor_scalar(out=ki, in0=x_ap, scalar1=1.0 / TWO_PI, scalar2=shift / TWO_PI,
                                              op0=ALU.mult, op1=ALU.add), reads=[x_res], writes=[r[0]])
    P.add("vector", lambda e: e.tensor_copy(out=kf, in_=ki), reads=[r[0]], writes=[r[1]])
    P.add("vector", lambda e: e.scalar_tensor_tensor(out=y, in0=kf, scalar=-C1, in1=x_ap, op0=ALU.mult, op1=ALU.add),
          reads=[r[1], x_res], writes=[r[2]])
    P.add("vector", lambda e: e.scalar_tensor_tensor(out=y, in0=kf, scalar=-C2, in1=y, op0=ALU.mult, op1=ALU.add),
          reads=[r[1], r[2]], writes=[r[2]])
    if shift != 0.0:
        P.add("vector", lambda e: e.tensor_scalar_add(out=y, in0=y, scalar1=float(shift)), reads=[r[2]], writes=[r[2]])
    P.add("vector", lambda e: e.tensor_single_scalar(out=m, in_=y, scalar=math.pi, op=ALU.is_gt), reads=[r[2]], writes=[r[3]])
    P.add("vector", lambda e: e.scalar_tensor_tensor(out=y, in0=m, scalar=-TWO_PI, in1=y, op0=ALU.mult, op1=ALU.add),
          reads=[r[3], r[2]], writes=[r[2]])
    P.add("vector", lambda e: e.tensor_single_scalar(out=m, in_=y, scalar=-math.pi, op=ALU.is_lt), reads=[r[2]], writes=[r[3]])
    P.add("vector", lambda e: e.scalar_tensor_tensor(out=y, in0=m, scalar=TWO_PI, in1=y, op0=ALU.mult, op1=ALU.add),
          reads=[r[3], r[2]], writes=[r[2]])
    P.add("vector", lambda e: e.tensor_scalar(out=y, in0=y, scalar1=-math.pi, scalar2=math.pi, op0=ALU.max, op1=ALU.min),
          reads=[r[2]], writes=[r[2]])
    P.add("scalar", lambda e: e.activation(out=out_ap, in_=y, func=AF.Sin), reads=[r[2]], writes=[out_res])


class PsPool:
    def __init__(self, names):
        self.names = list(names)
        self.i = 0

    def next(self):
        n = self.names[self.i % len(self.names)]
        self.i += 1
        return n


def emit_pnorm(P, srcs, np_, count, ones_ap, ones_res, sq_tiles, ps_ap, ps_res, rstd_ap, rstd_res, st):
    n = len(srcs)
    for i, (ap, res) in enumerate(srcs):
        s = st["sq"] % len(sq_tiles)
        st["sq"] += 1
        sqt = sq_tiles[s]
        P.add("scalar", lambda e, ap=ap, sqt=sqt: e.activation(out=sqt[0:np_, :], in_=ap, func=AF.Square),
              reads=[res], writes=["sq%d" % s])
        P.add("tensor", lambda e, i=i, sqt=sqt: e.matmul(ps_ap, ones_ap, sqt[0:np_, :], start=(i == 0), stop=(i == n - 1)),
              reads=[ones_res, "sq%d" % s], writes=[ps_res])
    P.add("scalar", lambda e: e.activation(out=rstd_ap, in_=ps_ap, func=AF.Sqrt, bias=EPS, scale=1.0 / count),
          reads=[ps_res], writes=[rstd_res])
    P.add("vector", lambda e: e.reciprocal(out=rstd_ap, in_=rstd_ap), reads=[rstd_res], writes=[rstd_res])


MLA_SCALE = 192 ** -0.5


def build_mla(S, debug=False):
    nc = bass.Bass("TRN2", target_bir_lowering=False)
    I32 = mybir.dt.int32
    NBK = S // 512
    NKB = S // 128
    DC = D // 128
    hT_d = nc.dram_tensor("hT", [D, S], BF16, kind="ExternalInput").ap()
    w_in_d = nc.dram_tensor("w_in", [D, 1088], F32, kind="ExternalInput").ap()
    wq_d = nc.dram_tensor("wq", [512, 384], F32, kind="ExternalInput").ap()
    wkv_d = nc.dram_tensor("wkv", [512, 512], F32, kind="ExternalInput").ap()
    latg_d = nc.dram_tensor("lat_g", [128, 8], F32, kind="ExternalInput").ap()
    hg_d = nc.dram_tensor("hg", [128, 6], F32, kind="ExternalInput").ap()
    pos_d = nc.dram_tensor("pos", [1, S], I32, kind="ExternalInput").ap()
    invf_d = nc.dram_tensor("invf", [32, 1], F32, kind="ExternalInput").ap()
    mask_d = nc.dram_tensor("mask", [128, 4 * 512], BF16, kind="ExternalInput").ap()
    out_d = nc.dram_tensor("mixT", [256, S], BF16, kind="ExternalOutput").ap()
    kw = dict(kind="ExternalOutput") if debug else {}
    qn_d = [nc.dram_tensor("qn_s%d" % h, [128, S], BF16, **kw).ap() for h in range(2)]
    qr_d = [nc.dram_tensor("qr_s%d" % h, [64, S], BF16, **kw).ap() for h in range(2)]
    kn_d = [nc.dram_tensor("kn_s%d" % h, [128, S], BF16, **kw).ap() for h in range(2)]
    kr_d = nc.dram_tensor("kr_s", [64, S], BF16, **kw).ap()
    v_d = [nc.dram_tensor("v_s%d" % h, [S, 128], BF16, **kw).ap() for h in range(2)]

    with ExitStack() as es:
        cx = Ctx(nc, es)
        P = Prog(nc)
        st = dict(sq=0)
        ones = cx.sb([128, 128], BF16, "ones")
        lat_g = cx.sb([128, 8], F32, "lat_g_sb")
        hg = cx.sb([128, 6], F32, "hg_sb")
        invs = cx.sb([32, 1], F32, "invs")
        maskt = cx.sb([128, 4, 512], BF16, "maskt")
        ps = [cx.ps([128, 512], F32, "ps%d" % i) for i in range(8)]
        psd = {"ps%d" % i: ps[i] for i in range(8)}
        P.add("gpsimd", lambda e: e.memset(ones[:], 1.0), writes=["ones"])
        P.add("sync", lambda e: e.dma_start(out=lat_g[:], in_=latg_d), writes=["lat_g"], dma="lat_g")
        P.add("sync", lambda e: e.dma_start(out=hg[:], in_=hg_d), writes=["hg"], dma="hg")
        P.add("sync", lambda e: e.dma_start(out=invs[:], in_=invf_d), writes=["invs"], dma="invs")
        P.add("sync", lambda e: e.dma_start(out=maskt[:], in_=mask_d.rearrange("p (o q) -> p o q", o=4)), writes=["maskt"], dma="maskt")

        with ExitStack() as esA:
            ca = Ctx(nc, esA)
            w_in = ca.sb([128, DC, 1088], BF16, "w_in_sb")
            wq = ca.sb([128, 4, 384], BF16, "wq_sb")
            wkv = ca.sb([128, 4, 512], BF16, "wkv_sb")
            hTb = [ca.sb([128, DC, 512], BF16, "hTb%d" % i) for i in range(2)]
            lat = ca.sb([128, 8, 512], F32, "lat")
            lnb = [ca.sb([128, 4, 512], BF16, "lnb%d" % i) for i in range(2)]
            sq = [ca.sb([128, 512], BF16, "sqa%d" % i) for i in range(2)]
            rstd = ca.sb([128, 512], F32, "rstd")
            rs32 = ca.sb([32, 512], F32, "rs32")
            pe = [ca.sb([32, 512], F32, "pe%d" % i) for i in range(2)]
            pn = [ca.sb([32, 512], F32, "pn%d" % i) for i in range(2)]
            tt = [ca.sb([32, 512], F32, "tt%d" % i) for i in range(2)]
            rot = [ca.sb([32, 512], BF16, "rot%d" % i) for i in range(2)]
            posi = ca.sb([32, 512], I32, "posi")
            ang = ca.sb([32, 512], F32, "ang")
            ki = ca.sb([32, 512], I32, "ki")
            kf = ca.sb([32, 512], F32, "kf")
            yy = ca.sb([32, 512], F32, "yy")
            mm_ = ca.sb([32, 512], F32, "mm_")
            cs = ca.sb([32, 512], F32, "cs")
            sn = ca.sb([32, 512], F32, "sn")
            nb = [ca.sb([128, 512], BF16, "nb%d" % i) for i in range(2)]
            vt = ca.sb([128, 4, 128], BF16, "vt")
            pp = PsPool(["ps%d" % i for i in range(8)])
            for k in range(4):
                P.add("gpsimd", lambda e, k=k: e.dma_start(out=w_in[:, 4 * k:4 * k + 4, :],
                                                           in_=w_in_d.rearrange("(c p) n -> p c n", p=128)[:, 4 * k:4 * k + 4, :]),
                      writes=["w_in%d" % k], dma="w_in%d" % k)
            P.add("gpsimd", lambda e: e.dma_start(out=wq[:], in_=wq_d.rearrange("(c p) n -> p c n", p=128)), writes=["wq"], dma="wq")
            P.add("gpsimd", lambda e: e.dma_start(out=wkv[:], in_=wkv_d.rearrange("(c p) n -> p c n", p=128)), writes=["wkv"], dma="wkv")
            w_in_res = ["w_in%d" % k for k in range(4)]

            def rope(a_res, gcol0, dst_d, tsl, rs_res):
                for i in range(2):
                    P.add("vector", lambda e, i=i: e.scalar_tensor_tensor(out=pn[i][:], in0=pe[i][:], scalar=hg[0:32, gcol0 + i:gcol0 + i + 1],
                                                                       in1=rs32[:], op0=ALU.mult, op1=ALU.mult),
                          reads=["pe%d" % i, "hg", rs_res], writes=["pn%d" % i])
                P.add("vector", lambda e: e.tensor_tensor(out=tt[0][:], in0=pn[0][:], in1=cs[:], op=ALU.mult), reads=["pn0", "cs"], writes=["tt0"])
                P.add("vector", lambda e: e.tensor_tensor(out=tt[1][:], in0=pn[1][:], in1=sn[:], op=ALU.mult), reads=["pn1", "sn"], writes=["tt1"])
                P.add("vector", lambda e: e.tensor_tensor(out=rot[0][:], in0=tt[0][:], in1=tt[1][:], op=ALU.subtract), reads=["tt0", "tt1"], writes=["rot0"])
                P.add("vector", lambda e: e.tensor_tensor(out=tt[0][:], in0=pn[1][:], in1=cs[:], op=ALU.mult), reads=["pn1", "cs", "rot0"], writes=["tt0"])
                P.add("vector", lambda e: e.tensor_tensor(out=tt[1][:], in0=pn[0][:], in1=sn[:], op=ALU.mult), reads=["pn0", "sn", "rot0"], writes=["tt1"])
                P.add("vector", lambda e: e.tensor_tensor(out=rot[1][:], in0=tt[0][:], in1=tt[1][:], op=ALU.add), reads=["tt0", "tt1"], writes=["rot1"])
                for i in range(2):
                    P.add("gpsimd", lambda e, i=i: e.dma_start(out=dst_d[32 * i:32 * i + 32, tsl], in_=rot[i][:]),
                          reads=["rot%d" % i], writes=[a_res], dma="st_rot%d" % i)

            def load_h(b):
                P.add("sync", lambda e, b=b: e.dma_start(out=hTb[b % 2][:], in_=hT_d.rearrange("(c p) t -> p c t", p=128)[:, :, b * 512:(b + 1) * 512]),
                      writes=["hTb%d" % (b % 2)], dma="hTb%d" % (b % 2))

            load_h(0)
            for b in range(NBK):
                tsl = slice(b * 512, (b + 1) * 512)
                if b + 1 < NBK:
                    load_h(b + 1)
                hb = hTb[b % 2]
                hres = "hTb%d" % (b % 2)
                for oc in range(8):
                    pr = pp.next()
                    for dc in range(DC):
                        P.add("tensor", lambda e, oc=oc, dc=dc, pr=pr, hb=hb: e.matmul(psd[pr][:], w_in[:, dc, oc * 128:(oc + 1) * 128], hb[:, dc, :],
                                                                             start=(dc == 0), stop=(dc == DC - 1)),
                              reads=[w_in_res[dc // 4], hres], writes=[pr])
                    if oc % 2 == 0:
                        P.add("scalar", lambda e, oc=oc, pr=pr: e.copy(out=lat[:, oc, :], in_=psd[pr][:]), reads=[pr], writes=["lat%d" % oc])
                    else:
                        P.add("vector", lambda e, oc=oc, pr=pr: e.tensor_copy(out=lat[:, oc, :], in_=psd[pr][:]), reads=[pr], writes=["lat%d" % oc])
                for i in range(2):
                    pr = pp.next()
                    for dc in range(DC):
                        P.add("tensor", lambda e, i=i, dc=dc, pr=pr, hb=hb: e.matmul(psd[pr][0:32, :], w_in[:, dc, 1024 + 32 * i:1056 + 32 * i], hb[:, dc, :],
                                                                            start=(dc == 0), stop=(dc == DC - 1)),
                              reads=[w_in_res[dc // 4], hres], writes=[pr])
                    P.add("vector", lambda e, i=i, pr=pr: e.tensor_copy(out=pe[i][:], in_=psd[pr][0:32, :]), reads=[pr], writes=["pe%d" % i])
                P.add("sync", lambda e, tsl=tsl: e.dma_start(out=posi[:], in_=pos_d[:, tsl].partition_broadcast(32)), writes=["posi"], dma="posi")
                P.add("vector", lambda e: e.tensor_copy(out=ang[:], in_=posi[:]), reads=["posi"], writes=["ang"])
                P.add("vector", lambda e: e.tensor_scalar(out=ang[:], in0=ang[:], scalar1=invs[:, 0:1], scalar2=None, op0=ALU.mult),
                      reads=["ang", "invs"], writes=["ang"])
                emit_sin(P, (ki[:], kf[:], yy[:], mm_[:]), ang[:], "ang", sn[:], "sn", 0.0, "sc")
                emit_sin(P, (ki[:], kf[:], yy[:], mm_[:]), ang[:], "ang", cs[:], "cs", float(np.pi / 2), "sc")
                pr = pp.next()
                emit_pnorm(P, [(pe[0][:], "pe0"), (pe[1][:], "pe1")], 32, 64, ones[0:32, 0:32], "ones", sq, psd[pr][0:32, :], pr, rs32[:], "rs32", st)
                rope("kr_d%d" % b, 4, kr_d, tsl, "rs32")
                for grp in range(2):
                    pr = pp.next()
                    emit_pnorm(P, [(lat[:, 4 * grp + c, :], "lat%d" % (4 * grp + c)) for c in range(4)], 128, 512, ones[:], "ones", sq,
                               psd[pr][:], pr, rstd[:], "rstd", st)
                    for c in range(4):
                        P.add("vector", lambda e, grp=grp, c=c: e.scalar_tensor_tensor(
                            out=lnb[grp][:, c, :], in0=lat[:, 4 * grp + c, :], scalar=lat_g[:, 4 * grp + c:4 * grp + c + 1], in1=rstd[:],
                            op0=ALU.mult, op1=ALU.mult),
                            reads=["lat%d" % (4 * grp + c), "lat_g", "rstd"], writes=["lnb%d" % grp])
                for h in range(2):
                    for which, (wt, wres, c0, dst, gcol) in enumerate(((wq, "wq", h * 192, qn_d[h], 0), (wkv, "wkv", h * 256, kn_d[h], 1))):
                        pr = pp.next()
                        for c in range(4):
                            P.add("tensor", lambda e, c=c, pr=pr, wt=wt, c0=c0, which=which: e.matmul(
                                psd[pr][:], wt[:, c, c0:c0 + 128], lnb[which][:, c, :], start=(c == 0), stop=(c == 3)),
                                reads=[wres, "lnb%d" % which], writes=[pr])
                        pr2 = pp.next()
                        emit_pnorm(P, [(psd[pr][:], pr)], 128, 128, ones[:], "ones", sq, psd[pr2][:], pr2, rstd[:], "rstd", st)
                        nbt = nb[which]
                        P.add("vector", lambda e, pr=pr, nbt=nbt, gcol=gcol: e.scalar_tensor_tensor(
                            out=nbt[:], in0=psd[pr][:], scalar=hg[:, gcol:gcol + 1], in1=rstd[:], op0=ALU.mult, op1=ALU.mult),
                            reads=[pr, "hg", "rstd"], writes=["nb%d" % which])
                        P.add("gpsimd", lambda e, nbt=nbt, dst=dst, tsl=tsl: e.dma_start(out=dst[:, tsl], in_=nbt[:]),
                              reads=["nb%d" % which], writes=["%s%d_d%d" % ("qn" if which == 0 else "kn", h, b)], dma="st_nb%d" % which)
                    for i in range(2):
                        pr = pp.next()
                        for c in range(4):
                            P.add("tensor", lambda e, c=c, pr=pr, i=i, h=h: e.matmul(
                                psd[pr][0:32, :], wq[:, c, h * 192 + 128 + 32 * i:h * 192 + 160 + 32 * i], lnb[0][:, c, :], start=(c == 0), stop=(c == 3)),
                                reads=["wq", "lnb0"], writes=[pr])
                        P.add("vector", lambda e, i=i, pr=pr: e.tensor_copy(out=pe[i][:], in_=psd[pr][0:32, :]), reads=[pr], writes=["pe%d" % i])
                    pr = pp.next()
                    emit_pnorm(P, [(pe[0][:], "pe0"), (pe[1][:], "pe1")], 32, 64, ones[0:32, 0:32], "ones", sq, psd[pr][0:32, :], pr, rs32[:], "rs32", st)
                    rope("qr%d_d%d" % (h, b), 2, qr_d[h], tsl, "rs32")
                    pr = pp.next()
                    for j in range(4):
                        for c in range(4):
                            P.add("tensor", lambda e, c=c, j=j, pr=pr, h=h: e.matmul(
                                psd[pr][:, j * 128:(j + 1) * 128], lnb[1][:, c, j * 128:(j + 1) * 128], wkv[:, c, h * 256 + 128:h * 256 + 256],
                                start=(c == 0), stop=(c == 3)),
                                reads=["wkv", "lnb1"], writes=[pr])
                    P.add("scalar", lambda e, pr=pr: e.copy(out=vt[:].rearrange("p j d -> p (j d)"), in_=psd[pr][:]), reads=[pr], writes=["vt"])
                    P.add("gpsimd", lambda e, h=h, b=b: e.dma_start(out=v_d[h][b * 512:(b + 1) * 512, :].rearrange("(j p) d -> p j d", p=128), in_=vt[:]),
                          reads=["vt"], writes=["v%d_d%d" % (h, b)], dma="st_vt")

        P.barrier()
        with ExitStack() as esB:
            cb = Ctx(nc, esB)
            knT = cb.sb([128, S], BF16, "knT")
            krT = cb.sb([64, S], BF16, "krT")
            V = cb.sb([128, NKB, 128], BF16, "V")
            qn = [cb.sb([128, 512], BF16, "qn%d" % i) for i in range(2)]
            qr = [cb.sb([64, 512], BF16, "qr%d" % i) for i in range(2)]
            NPT = 3
            pT = [cb.sb([128, 512], BF16, "pT%d" % i) for i in range(NPT)]
            rl = cb.sb([128, 512], F32, "rl")
            oT = [cb.sb([128, 512], BF16, "oT%d" % i) for i in range(2)]
            sp = PsPool(["ps0", "ps1", "ps2", "ps3"])
            all_scr = lambda pfx: [pfx + "_d%d" % b for b in range(NBK)]
            P.add("sync", lambda e: e.dma_start(out=krT[:], in_=kr_d), reads=all_scr("kr"), writes=["krT"], dma="krT")
            cnt = 0
            qi = 0
            for h in range(2):
                P.add("sync", lambda e, h=h: e.dma_start(out=knT[:], in_=kn_d[h]), reads=all_scr("kn%d" % h), writes=["knT"], dma="knT")
                P.add("sync", lambda e, h=h: e.dma_start(out=V[:], in_=v_d[h].rearrange("(n p) d -> p n d", p=128)),
                      reads=all_scr("v%d" % h), writes=["V"], dma="V")
                for Q in range(NBK):
                    tsl = slice(Q * 512, (Q + 1) * 512)
                    qs = qi % 2
                    qi += 1
                    P.add("sync", lambda e, h=h, tsl=tsl, qs=qs: e.dma_start(out=qn[qs][:], in_=qn_d[h][:, tsl]),
                          reads=all_scr("qn%d" % h), writes=["qn%d" % qs], dma="qn%d" % qs)
                    P.add("sync", lambda e, h=h, tsl=tsl, qs=qs: e.dma_start(out=qr[qs][:], in_=qr_d[h][:, tsl]),
                          reads=all_scr("qr%d" % h), writes=["qr%d" % qs], dma="qr%d" % qs)
                    po = "ps%d" % (4 + qs)
                    pl = "ps%d" % (6 + qs)
                    nkb = 4 * Q + 4
                    for kb in range(nkb):
                        ksl = slice(kb * 128, (kb + 1) * 128)
                        pS = sp.next()
                        P.add("tensor", lambda e, ksl=ksl, pS=pS, qs=qs: e.matmul(psd[pS][:], knT[:, ksl], qn[qs][:], start=True, stop=False),
                              reads=["knT", "qn%d" % qs], writes=[pS])
                        P.add("tensor", lambda e, ksl=ksl, pS=pS, qs=qs: e.matmul(psd[pS][:], krT[:, ksl], qr[qs][:], start=False, stop=True),
                              reads=["krT", "qr%d" % qs], writes=[pS])
                        pt = cnt % NPT
                        cnt += 1
                        P.add("scalar", lambda e, pS=pS, pt=pt: e.activation(out=pT[pt][:], in_=psd[pS][:], func=AF.Exp, scale=MLA_SCALE),
                              reads=[pS], writes=["pT%d" % pt])
                        if kb >= 4 * Q:
                            o = kb - 4 * Q
                            P.add("vector", lambda e, pt=pt, o=o: e.tensor_tensor(out=pT[pt][:], in0=pT[pt][:], in1=maskt[:, o, :], op=ALU.mult),
                                  reads=["pT%d" % pt, "maskt"], writes=["pT%d" % pt])
                        P.add("tensor", lambda e, kb=kb, pt=pt, po=po, nkb=nkb: e.matmul(psd[po][:], V[:, kb, :], pT[pt][:], start=(kb == 0), stop=(kb == nkb - 1)),
                              reads=["V", "pT%d" % pt], writes=[po])
                        P.add("tensor", lambda e, kb=kb, pt=pt, pl=pl, nkb=nkb: e.matmul(psd[pl][:], ones[:], pT[pt][:], start=(kb == 0), stop=(kb == nkb - 1)),
                              reads=["ones", "pT%d" % pt], writes=[pl])
                    P.add("vector", lambda e, pl=pl: e.reciprocal(out=rl[:], in_=psd[pl][:]), reads=[pl], writes=["rl"])
                    P.add("vector", lambda e, po=po, qs=qs: e.tensor_tensor(out=oT[qs][:], in0=psd[po][:], in1=rl[:], op=ALU.mult),
                          reads=[po, "rl"], writes=["oT%d" % qs])
                    P.add("gpsimd", lambda e, h=h, tsl=tsl, qs=qs: e.dma_start(out=out_d[h * 128:(h + 1) * 128, tsl], in_=oT[qs][:]),
                          reads=["oT%d" % qs], writes=["out%d_%d" % (h, Q)], dma="st_o%d" % qs)
            P.finish(["out%d_%d" % (h, Q) for h in range(2) for Q in range(NBK)])
        P.emit()
    return nc


def mla_consts():
    invf = (10000.0 ** (-np.arange(0, 64, 2, dtype=np.float32) / 64)).astype(np.float32)[:, None]
    ki = np.arange(128)[:, None]
    qi = np.arange(512)[None, :]
    mask = np.concatenate([(qi >= o * 128 + ki) for o in range(4)], axis=1).astype(np.float32).astype(ml_dtypes.bfloat16)
    return np.ascontiguousarray(invf), np.ascontiguousarray(mask)


def run_mla(hT, positions, w_in, q_norm, kv_norm, w_q_up, w_kv_up, q_head_norm, k_head_norm, debug=False):
    S = hT.shape[1]
    key = ("mla", S, debug)
    if key not in _NC_CACHE:
        _NC_CACHE[key] = build_mla(S, debug)
    nc = _NC_CACHE[key]
    invf, mask = mla_consts()
    lat_g = np.concatenate([np.asarray(q_norm, np.float32).reshape(4, 128).T, np.asarray(kv_norm, np.float32).reshape(4, 128).T], axis=1)
    hg = np.zeros((128, 6), np.float32)
    qh = np.asarray(q_head_norm, np.float32)
    kh = np.asarray(k_head_norm, np.float32)
    hg[:, 0] = qh[0:128]
    hg[:, 1] = kh[0:128]
    hg[0:32, 2] = qh[128:160]
    hg[0:32, 3] = qh[160:192]
    hg[0:32, 4] = kh[128:160]
    hg[0:32, 5] = kh[160:192]
    pos = np.ascontiguousarray(np.asarray(positions, np.int32).reshape(1, S))
    in_maps = []
    for c in range(NCORES):
        wq = np.ascontiguousarray(np.asarray(w_q_up)[:, c * 384:(c + 1) * 384])
        wkv = np.ascontiguousarray(np.asarray(w_kv_up)[:, c * 512:(c + 1) * 512])
        in_maps.append(dict(hT=hT, w_in=np.asarray(w_in), wq=wq, wkv=wkv, lat_g=np.ascontiguousarray(lat_g), hg=hg,
                            pos=pos, invf=invf, mask=mask))
    res = run_bass_kernel_spmd(nc, in_maps, core_ids=list(range(NCORES)))
    if debug:
        return res.results
    return np.concatenate([r["mixT"] for r in res.results], axis=0)


def e_mm(P, out, lhsT, rhs, start, stop, r, w):
    P.add("tensor", lambda e: e.matmul(out, lhsT, rhs, start=start, stop=stop), reads=r, writes=w)


def e_tr(P, out, in_, ident, r, w):
    P.add("tensor", lambda e: e.transpose(out, in_, ident), reads=r, writes=w)


def e_act(P, out, in_, func, r, w, bias=None, scale=None):
    kw = {}
    if bias is not None:
        kw["bias"] = bias
    if scale is not None:
        kw["scale"] = scale
    P.add("scalar", lambda e: e.activation(out=out, in_=in_, func=func, **kw), reads=r, writes=w)


def e_tt(P, out, in0, in1, op, r, w, eng="vector"):
    P.add(eng, lambda e: e.tensor_tensor(out=out, in0=in0, in1=in1, op=op), reads=r, writes=w)


def e_ts(P, out, in0, s1, s2, op0, op1, r, w, eng="vector"):
    if s2 is None:
        P.add(eng, lambda e: e.tensor_scalar(out=out, in0=in0, scalar1=s1, scalar2=None, op0=op0), reads=r, writes=w)
    else:
        P.add(eng, lambda e: e.tensor_scalar(out=out, in0=in0, scalar1=s1, scalar2=s2, op0=op0, op1=op1), reads=r, writes=w)


def e_stt(P, out, in0, scalar, in1, op0, op1, r, w, eng="vector"):
    P.add(eng, lambda e: e.scalar_tensor_tensor(out=out, in0=in0, scalar=scalar, in1=in1, op0=op0, op1=op1), reads=r, writes=w)


def e_cp(P, out, in_, r, w, eng="vector"):
    if eng == "scalar":
        P.add("scalar", lambda e: e.copy(out=out, in_=in_), reads=r, writes=w)
    else:
        P.add(eng, lambda e: e.tensor_copy(out=out, in_=in_), reads=r, writes=w)


def e_dma(P, q, out, in_, r, w, key):
    P.add(q, lambda e: e.dma_start(out=out, in_=in_), reads=r, writes=w, dma=key)


GW = 770


def build_gdn(S):
    nc = bass.Bass("TRN2", target_bir_lowering=False)
    NBK = S // 512
    DC = D // 128
    hT_d = nc.dram_tensor("hT", [D, S], BF16, kind="ExternalInput").ap()
    w_d = nc.dram_tensor("w", [D, GW], F32, kind="ExternalInput").ap()
    cw_d = nc.dram_tensor("cw", [128, 12], F32, kind="ExternalInput").ap()
    par_d = nc.dram_tensor("par", [64, 2], F32, kind="ExternalInput").ap()
    gn_d = nc.dram_tensor("gn", [64, 128], F32, kind="ExternalInput").ap()
    pw_d = nc.dram_tensor("pw", [256, 128], F32, kind="ExternalInput").ap()
    psc_d = nc.dram_tensor("psc", [128, 1], F32, kind="ExternalInput").ap()
    cmat_d = nc.dram_tensor("cmat", [128, 64], F32, kind="ExternalInput").ap()
    cst_d = nc.dram_tensor("cst", [64, 256], F32, kind="ExternalInput").ap()
    out_d = nc.dram_tensor("mixT", [256, S], BF16, kind="ExternalOutput").ap()

    with ExitStack() as es:
        cx = Ctx(nc, es)
        P = Prog(nc)
        st = dict(sq=0)
        f = lambda shape, name: cx.sb(shape, F32, name)
        w_sb = cx.sb([128, DC, GW], BF16, "w_sb")
        hTb = [cx.sb([128, DC, 512], BF16, "hTb%d" % i) for i in range(2)]
        cw = f([128, 12], "cw_sb")
        par = f([64, 2], "par_sb")
        gn = f([64, 128], "gn_sb")
        pw = cx.sb([128, 2, 128], BF16, "pw_sb")
        psc = f([128, 1], "psc_sb")
        cmat = f([128, 4, 16], "cmat_sb")
        cst = f([64, 4, 64], "cst_sb")
        ident = f([128, 128], "ident")
        ones_b = cx.sb([128, 128], BF16, "ones_b")
        ones_f = f([64, 128], "ones_f")
        Aexp = f([64, 1], "Aexp")
        cpad = [f([128, 515], "cpad%d" % i) for i in range(3)]
        cv = [f([128, 512], "cv%d" % i) for i in range(3)]
        sl = [f([128, 512], "sl%d" % i) for i in range(3)]
        sq = [cx.sb([128, 512], BF16, "sqg%d" % i) for i in range(2)]
        rstd = f([128, 512], "rstd")
        qT = f([128, 512], "qT")
        kT = f([128, 512], "kT")
        qdT = f([128, 512], "qdT")
        ktm = f([64, 8, 128], "ktm")
        vtm = f([64, 8, 128], "vtm")
        zab = f([64, 8, 130], "zab")
        e1 = f([64, 8], "e1")
        gg = f([64, 8], "gg")
        beta = f([64, 8], "beta")
        gc = f([64, 8], "gc")
        egc = f([64, 8], "egc")
        ekd = f([64, 8], "ekd")
        egl = f([128, 8], "egl")
        bgc = f([64, 8], "bgc")
        diag = f([64, 8, 64], "diag")
        E = f([64, 8, 64], "E")
        NEB = f([64, 8, 64], "NEB")
        PT = [f([64, 8, 64], "PT%d" % i) for i in range(2)]
        Pm = [f([64, 8, 64], "Pm%d" % i) for i in range(2)]
        R = f([64, 8, 64], "R")
        attnT = f([64, 8, 64], "attnT")
        vb = f([64, 8, 128], "vb")
        kbg = f([64, 8, 128], "kbg")
        kd = f([64, 8, 128], "kd")
        u_sb = f([64, 8, 128], "u_sb")
        wT = f([128, 512], "wT")
        Sst = f([128, 128], "Sst")
        vnew = [f([64, 128], "vnew%d" % i) for i in range(2)]
        obuf = f([64, 8, 128], "obuf")
        osq = f([64, 8, 128], "osq")
        oss = f([64, 8], "oss")
        zs = f([64, 8, 128], "zs")
        oT = cx.sb([128, 512], BF16, "oT")
        upad = [f([128, 528], "upad%d" % i) for i in range(2)]
        sA = [f([128, 528], "sA%d" % i) for i in range(2)]
        sB = [f([128, 528], "sB%d" % i) for i in range(2)]
        pacc = [f([128, 528], "pacc%d" % i) for i in range(2)]
        dif = [cx.sb([128, 512], BF16, "dif%d" % i) for i in range(2)]
        yT = cx.sb([128, 512], BF16, "yT")
        ps = [cx.ps([128, 512], F32, "ps%d" % i) for i in range(8)]
        psd = {"ps%d" % i: ps[i] for i in range(8)}
        pp = PsPool(["ps%d" % i for i in range(8)])

        for k in range(4):
            e_dma(P, "gpsimd", w_sb[:, 4 * k:4 * k + 4, :], w_d.rearrange("(c p) n -> p c n", p=128)[:, 4 * k:4 * k + 4, :], [], ["w_sb%d" % k], "w_sb%d" % k)
        wres = ["w_sb%d" % k for k in range(4)]
        e_dma(P, "gpsimd", pw[:], pw_d.rearrange("(c p) n -> p c n", p=128), [], ["pw"], "pw")
        e_dma(P, "sync", cw[:], cw_d, [], ["cw"], "cw")
        e_dma(P, "sync", par[:], par_d, [], ["par"], "par")
        e_dma(P, "sync", gn[:], gn_d, [], ["gn"], "gn")
        e_dma(P, "sync", psc[:], psc_d, [], ["psc"], "psc")
        e_dma(P, "sync", cmat[:], cmat_d.rearrange("p (w t) -> p w t", w=4), [], ["cmat"], "cmat")
        e_dma(P, "sync", cst[:], cst_d.rearrange("p (w t) -> p w t", w=4), [], ["cst"], "cst")
        P.add("gpsimd", lambda e: e.memset(ones_b[:], 1.0), writes=["ones_b"])
        P.add("gpsimd", lambda e: e.memset(ones_f[:], 1.0), writes=["ones_f"])
        P.add("gpsimd", lambda e: e.memset(ident[:], 0.0), writes=["ident"])
        P.add("gpsimd", lambda e: e.affine_select(out=ident[:], in_=ident[:], pattern=[[-1, 128]], compare_op=ALU.not_equal, fill=1.0,
                                                  base=0, channel_multiplier=1), reads=["ident"], writes=["ident"])
        P.add("gpsimd", lambda e: e.memset(Sst[:], 0.0), writes=["Sst"])
        for i in range(3):
            P.add("gpsimd", lambda e, i=i: e.memset(cpad[i][:, 0:3], 0.0), writes=["cpad%d" % i])
        for i in range(2):
            P.add("gpsimd", lambda e, i=i: e.memset(upad[i][:, 0:16], 0.0), writes=["upad%d" % i])
        e_act(P, Aexp[:], par[:, 0:1], AF.Exp, ["par"], ["Aexp"])
        Ltri = cst[:, 0, :]
        SM = cst[:, 1, :]
        MN = cst[:, 2, :]
        I64 = cst[:, 3, :]

        def bc_c(ap2):
            return ap2.unsqueeze(1).to_broadcast([64, 8, 64])

        def bc_i(col, n):
            return col.unsqueeze(2).to_broadcast([64, 8, n])

        def flat(t):
            return t[:].rearrange("p c i -> p (c i)")

        def rowbcast(col_ap, col_res, m, pr):
            e_tt(P, diag[:], bc_c(I64), bc_i(col_ap, 64), ALU.mult, ["cst", col_res], ["diag"])
            e_mm(P, psd[pr][0:m, :], ones_f[:, 0:m], flat(diag), True, True, ["ones_f", "diag"], [pr])

        def load_h(b):
            e_dma(P, "sync", hTb[b % 2][:], hT_d.rearrange("(c p) t -> p c t", p=128)[:, :, b * 512:(b + 1) * 512], [], ["hTb%d" % (b % 2)], "hTb%d" % (b % 2))

        load_h(0)
        for b in range(NBK):
            tsl = slice(b * 512, (b + 1) * 512)
            if b + 1 < NBK:
                load_h(b + 1)
            hb = hTb[b % 2]
            hres = "hTb%d" % (b % 2)
            for wh in range(3):
                pr = pp.next()
                for dc in range(DC):
                    e_mm(P, psd[pr][:], w_sb[:, dc, wh * 128:(wh + 1) * 128], hb[:, dc, :], dc == 0, dc == DC - 1, [wres[dc // 4], hres], [pr])
                e_cp(P, cpad[wh][:, 3:515], psd[pr][:], [pr], ["cpad%d" % wh], eng="scalar")
                e_ts(P, cv[wh][:], cpad[wh][:, 3:515], cw[:, wh * 4 + 3:wh * 4 + 4], None, ALU.mult, None, ["cpad%d" % wh, "cw"], ["cv%d" % wh])
                for j in (2, 1, 0):
                    e_stt(P, cv[wh][:], cpad[wh][:, j:j + 512], cw[:, wh * 4 + j:wh * 4 + j + 1], cv[wh][:], ALU.mult, ALU.add,
                          ["cpad%d" % wh, "cw", "cv%d" % wh], ["cv%d" % wh])
                e_cp(P, cpad[wh][:, 0:3], cpad[wh][:, 512:515], ["cpad%d" % wh], ["cpad%d" % wh], eng="gpsimd")
                e_act(P, sl[wh][:], cv[wh][:], AF.Silu, ["cv%d" % wh], ["sl%d" % wh])
            pr = pp.next()
            emit_pnorm(P, [(sl[0][:], "sl0")], 128, 1.0, ones_b[:], "ones_b", sq, psd[pr][:], pr, rstd[:], "rstd", st)
            e_stt(P, qT[:], sl[0][:], float(128 ** -0.5), rstd[:], ALU.mult, ALU.mult, ["sl0", "rstd"], ["qT"])
            pr = pp.next()
            emit_pnorm(P, [(sl[1][:], "sl1")], 128, 1.0, ones_b[:], "ones_b", sq, psd[pr][:], pr, rstd[:], "rstd", st)
            e_tt(P, kT[:], sl[1][:], rstd[:], ALU.mult, ["sl1", "rstd"], ["kT"])
            for (src, sres, dst, dres) in ((kT, "kT", ktm, "ktm"), (sl[2], "sl2", vtm, "vtm")):
                for half in range(2):
                    pr = pp.next()
                    for cc in range(4):
                        c = half * 4 + cc
                        e_tr(P, psd[pr][0:64, cc * 128:(cc + 1) * 128], src[:, c * 64:(c + 1) * 64], ident[:], [sres, "ident"], [pr])
                    e_cp(P, dst[:, half * 4:half * 4 + 4, :].rearrange("p c d -> p (c d)"), psd[pr][0:64, :], [pr], [dres], eng="scalar" if half else "vector")
            for half in range(4):
                pr = pp.next()
                for cc in range(2):
                    c = half * 2 + cc
                    for dc in range(DC):
                        e_mm(P, psd[pr][0:64, cc * 130:(cc + 1) * 130], hb[:, dc, c * 64:(c + 1) * 64], w_sb[:, dc, 384:514], dc == 0, dc == DC - 1,
                             [wres[dc // 4], hres], [pr])
                e_cp(P, zab[:, half * 2:half * 2 + 2, :].rearrange("p c d -> p (c d)"), psd[pr][0:64, 0:260], [pr], ["zab"], eng="scalar" if half % 2 else "vector")
            e_act(P, e1[:], zab[:, :, 128], AF.Exp, ["zab", "par"], ["e1"], bias=par[:, 1:2])
            e_act(P, e1[:], e1[:], AF.Ln, ["e1"], ["e1"], bias=1.0)
            e_ts(P, gg[:], e1[:], Aexp[:, 0:1], -1.0, ALU.mult, ALU.mult, ["e1", "Aexp"], ["gg"])
            e_act(P, beta[:], zab[:, :, 129], AF.Sigmoid, ["zab"], ["beta"])
            e_act(P, zs[:], zab[:, :, 0:128], AF.Silu, ["zab"], ["zs"])
            pr = pp.next()
            e_mm(P, psd[pr][0:64, 0:8], Ltri, gg[:], True, True, ["cst", "gg"], [pr])
            e_cp(P, gc[:], psd[pr][0:64, 0:8], [pr], ["gc"])
            pr = pp.next()
            e_mm(P, psd[pr][:, 0:8], ones_f[:, :], gg[:], True, True, ["ones_f", "gg"], [pr])
            e_act(P, egl[:], psd[pr][:, 0:8], AF.Exp, [pr], ["egl"])
            e_tt(P, ekd[:], psd[pr][0:64, 0:8], gc[:], ALU.subtract, [pr, "gc"], ["ekd"])
            e_act(P, ekd[:], ekd[:], AF.Exp, ["ekd"], ["ekd"])
            e_act(P, egc[:], gc[:], AF.Exp, ["gc"], ["egc"])
            e_tt(P, bgc[:], beta[:], egc[:], ALU.mult, ["beta", "egc"], ["bgc"])
            pr = pp.next()
            rowbcast(gc[:], "gc", 64, pr)
            e_tt(P, E[:], psd[pr][0:64, :].rearrange("p (c i) -> p c i", c=8), bc_i(gc[:], 64), ALU.subtract, [pr, "gc"], ["E"])
            e_tt(P, E[:], E[:], bc_c(MN), ALU.add, ["E", "cst"], ["E"])
            e_act(P, E[:], E[:], AF.Exp, ["E"], ["E"])
            pr = pp.next()
            rowbcast(beta[:], "beta", 64, pr)
            e_tt(P, NEB[:], psd[pr][0:64, :].rearrange("p (c i) -> p c i", c=8), E[:], ALU.mult, [pr, "E"], ["NEB"])
            e_stt(P, NEB[:], NEB[:], -1.0, bc_c(SM), ALU.mult, ALU.mult, ["NEB", "cst"], ["NEB"])
            pr = pp.next()
            rowbcast(egc[:], "egc", 128, pr)
            e_tt(P, qdT[:], qT[:], psd[pr][:], ALU.mult, ["qT", pr], ["qdT"])
            prk = pp.next()
            prq = pp.next()
            for c in range(8):
                cs_ = slice(c * 64, (c + 1) * 64)
                e_mm(P, psd[prk][0:64, cs_], kT[:, cs_], kT[:, cs_], True, True, ["kT"], [prk])
                e_mm(P, psd[prq][0:64, cs_], kT[:, cs_], qT[:, cs_], True, True, ["kT", "qT"], [prq])
            e_tt(P, flat(PT[0]), psd[prk][0:64, :], flat(NEB), ALU.mult, [prk, "NEB"], ["PT0"])
            e_tt(P, flat(attnT), psd[prq][0:64, :], flat(E), ALU.mult, [prq, "E"], ["attnT"])
            pr = pp.next()
            for c in range(8):
                e_tr(P, psd[pr][0:64, c * 64:(c + 1) * 64], PT[0][:, c, :], ident[0:64, 0:64], ["PT0", "ident"], [pr])
            e_cp(P, flat(Pm[0]), psd[pr][0:64, :], [pr], ["Pm0"], eng="scalar")
            e_tt(P, R[:], PT[0][:], bc_c(I64), ALU.add, ["PT0", "cst"], ["R"])
            cur = 0
            for lvl in range(1, 6):
                nxt = 1 - cur
                pa = pp.next()
                for c in range(8):
                    e_mm(P, psd[pa][0:64, c * 64:(c + 1) * 64], PT[cur][:, c, :], Pm[cur][:, c, :], True, True, ["PT%d" % cur, "Pm%d" % cur], [pa])
                e_cp(P, flat(Pm[nxt]), psd[pa][0:64, :], [pa], ["Pm%d" % nxt], eng="scalar")
                if lvl < 5:
                    pb = pp.next()
                    for c in range(8):
                        e_mm(P, psd[pb][0:64, c * 64:(c + 1) * 64], Pm[cur][:, c, :], PT[cur][:, c, :], True, True, ["PT%d" % cur, "Pm%d" % cur], [pb])
                    e_cp(P, flat(PT[nxt]), psd[pb][0:64, :], [pb], ["PT%d" % nxt])
                pc = pp.next()
                for c in range(8):
                    e_mm(P, psd[pc][0:64, c * 64:(c + 1) * 64], Pm[nxt][:, c, :], R[:, c, :], True, True, ["Pm%d" % nxt, "R"], [pc])
                e_tt(P, flat(R), flat(R), psd[pc][0:64, :], ALU.add, ["R", pc], ["R"])
                cur = nxt
            e_tt(P, vb[:], vtm[:], bc_i(beta[:], 128), ALU.mult, ["vtm", "beta"], ["vb"])
            e_tt(P, kbg[:], ktm[:], bc_i(bgc[:], 128), ALU.mult, ["ktm", "bgc"], ["kbg"])
            e_tt(P, kd[:], ktm[:], bc_i(ekd[:], 128), ALU.mult, ["ktm", "ekd"], ["kd"], eng="gpsimd")
            for half in range(2):
                pr = pp.next()
                for cc in range(4):
                    c = half * 4 + cc
                    e_mm(P, psd[pr][0:64, cc * 128:(cc + 1) * 128], R[:, c, :], vb[:, c, :], True, True, ["R", "vb"], [pr])
                e_cp(P, u_sb[:, half * 4:half * 4 + 4, :].rearrange("p c d -> p (c d)"), psd[pr][0:64, :], [pr], ["u_sb"], eng="scalar" if half else "vector")
            pr = pp.next()
            for c in range(8):
                e_mm(P, psd[pr][:, c * 64:(c + 1) * 64], kbg[:, c, :], R[:, c, :], True, True, ["kbg", "R"], [pr])
            e_cp(P, wT[:], psd[pr][:], [pr], ["wT"], eng="scalar")
            for c in range(8):
                cs_ = slice(c * 64, (c + 1) * 64)
                vn = vnew[c % 2]
                vres = "vnew%d" % (c % 2)
                p1 = pp.next()
                e_mm(P, psd[p1][0:64, 0:128], wT[:, cs_], Sst[:], True, True, ["wT", "Sst"], [p1])
                e_tt(P, vn[:], u_sb[:, c, :], psd[p1][0:64, 0:128], ALU.subtract, ["u_sb", p1], [vres])
                p2 = pp.next()
                e_mm(P, psd[p2][0:64, 0:128], qdT[:, cs_], Sst[:], True, False, ["qdT", "Sst"], [p2])
                e_mm(P, psd[p2][0:64, 0:128], attnT[:, c, :], vn[:], False, True, ["attnT", vres], [p2])
                e_cp(P, obuf[:, c, :], psd[p2][0:64, 0:128], [p2], ["obuf"], eng="scalar")
                p3 = pp.next()
                e_mm(P, psd[p3][:, 0:128], kd[:, c, :], vn[:], True, True, ["kd", vres], [p3])
                e_stt(P, Sst[:], Sst[:], egl[:, c:c + 1], psd[p3][:, 0:128], ALU.mult, ALU.add, ["Sst", "egl", p3], ["Sst"])
            e_tt(P, osq[:], obuf[:], obuf[:], ALU.mult, ["obuf"], ["osq"], eng="gpsimd")
            P.add("vector", lambda e: e.tensor_reduce(out=oss[:], in_=osq[:], axis=AX.X, op=ALU.add), reads=["osq"], writes=["oss"])
            e_act(P, oss[:], oss[:], AF.Sqrt, ["oss"], ["oss"], bias=EPS, scale=1.0 / 128)
            P.add("vector", lambda e: e.reciprocal(out=oss[:], in_=oss[:]), reads=["oss"], writes=["oss"])
            e_tt(P, obuf[:], obuf[:], bc_i(oss[:], 128), ALU.mult, ["obuf", "oss"], ["obuf"])
            e_tt(P, obuf[:], obuf[:], gn[:].unsqueeze(1).to_broadcast([64, 8, 128]), ALU.mult, ["obuf", "gn"], ["obuf"])
            e_tt(P, obuf[:], obuf[:], zs[:], ALU.mult, ["obuf", "zs"], ["obuf"])
            pr = pp.next()
            for c in range(8):
                e_tr(P, psd[pr][:, c * 64:(c + 1) * 64], obuf[:, c, :], ident[0:64, 0:64], ["obuf", "ident"], [pr])
            e_cp(P, oT[:], psd[pr][:], [pr], ["oT"], eng="scalar")
            e_dma(P, "gpsimd", out_d[0:128, tsl], oT[:], ["oT"], ["out_o%d" % b], "st_oT")
            for ch in range(2):
                pr = pp.next()
                for dc in range(DC):
                    e_mm(P, psd[pr][:], w_sb[:, dc, 514 + ch * 128:514 + (ch + 1) * 128], hb[:, dc, :], dc == 0, dc == DC - 1, [wres[dc // 4], hres], [pr])
                up = upad[ch]
                ur = "upad%d" % ch
                e_cp(P, up[:, 16:528], psd[pr][:], [pr], [ur], eng="scalar")
                a_, b_ = sA[ch], sB[ch]
                ar, br = "sA%d" % ch, "sB%d" % ch
                pa_, par_ = pacc[ch], "pacc%d" % ch
                eng = "vector"
                lvl_src = [(up, ur, a_, ar, 1), (a_, ar, b_, br, 2), (b_, br, a_, ar, 4), (a_, ar, b_, br, 8)]
                for wi, (s_, sr, d_, dr, sh) in enumerate(lvl_src):
                    lo = 2 * sh - 1
                    e_tt(P, d_[:, lo:528], s_[:, lo:528], s_[:, lo - sh:528 - sh], ALU.add, [sr], [dr], eng=eng)
                    if wi == 0:
                        e_ts(P, pa_[:, 16:528], d_[:, 16:528], cmat[:, wi, 15:16], None, ALU.mult, None, [dr, "cmat"], [par_], eng=eng)
                    else:
                        e_stt(P, pa_[:, 16:528], d_[:, 16:528], cmat[:, wi, 15:16], pa_[:, 16:528], ALU.mult, ALU.add, [dr, "cmat", par_], [par_], eng=eng)
                    if b == 0:
                        tmp = sB[ch][:, 0:16] if (wi % 2 == 0) else sA[ch][:, 0:16]
                        tres = br if (wi % 2 == 0) else ar
                        e_tt(P, tmp, d_[:, 16:32], cmat[:, wi, :], ALU.mult, [dr, "cmat", tres], [tres], eng=eng)
                        if wi == 0:
                            e_cp(P, pa_[:, 0:16], tmp, [tres, par_], [par_], eng=eng)
                        else:
                            e_tt(P, pa_[:, 0:16], pa_[:, 0:16], tmp, ALU.add, [tres, par_], [par_], eng=eng)
                if b == 0:
                    e_cp(P, pa_[:, 16:32], pa_[:, 0:16], [par_], [par_], eng=eng)
                e_tt(P, dif[ch][:], pa_[:, 16:528], up[:, 16:528], ALU.subtract, [par_, ur], ["dif%d" % ch], eng=eng)
                e_cp(P, up[:, 0:16], up[:, 512:528], [ur, ar, br], [ur], eng=eng)
            pr = pp.next()
            for ch in range(2):
                e_mm(P, psd[pr][:], pw[:, ch, :], dif[ch][:], ch == 0, ch == 1, ["pw", "dif%d" % ch], [pr])
            e_ts(P, yT[:], psd[pr][:], psc[:, 0:1], None, ALU.mult, None, [pr, "psc"], ["yT"])
            e_dma(P, "gpsimd", out_d[128:256, tsl], yT[:], ["yT"], ["out_p%d" % b], "st_yT")
        P.finish(["out_o%d" % b for b in range(NBK)] + ["out_p%d" % b for b in range(NBK)])
        P.emit()
    return nc


POOL_WINDOWS = (2, 4, 8, 16)


def gdn_consts():
    t = np.arange(64)
    Ltri = (t[:, None] <= t[None, :]).astype(np.float32)
    SM = (t[None, :] > t[:, None]).astype(np.float32)
    MN = np.where(t[None, :] >= t[:, None], 0.0, -1e4).astype(np.float32)
    I64 = np.eye(64, dtype=np.float32)
    return np.ascontiguousarray(np.concatenate([Ltri, SM, MN, I64], axis=1))


def run_gdn(hT, w_in, conv_w, a_log, dt_bias, out_norm, pool_w, pool_scale):
    S = hT.shape[1]
    key = ("gdn", S)
    if key not in _NC_CACHE:
        _NC_CACHE[key] = build_gdn(S)
    nc = _NC_CACHE[key]
    w_in = np.asarray(w_in)
    conv_w = np.asarray(conv_w)
    cst = gdn_consts()
    in_maps = []
    for c in range(NCORES):
        g, hf = c // 2, c % 2
        cols = np.concatenate([np.arange(c * 128, (c + 1) * 128), 1024 + np.arange(c * 128, (c + 1) * 128), 2048 + np.arange(c * 128, (c + 1) * 128),
                               3072 + np.arange(c * 128, (c + 1) * 128), [4096 + c], [4104 + c], 4112 + np.arange(g * 256, (g + 1) * 256)])
        w = np.ascontiguousarray(w_in[:, cols])
        cw = np.zeros((128, 12), np.float32)
        for wh in range(3):
            cw[:, wh * 4:(wh + 1) * 4] = conv_w[:, wh * 1024 + c * 128: wh * 1024 + (c + 1) * 128].T
        par = np.zeros((64, 2), np.float32)
        par[:, 0] = np.asarray(a_log)[c]
        par[:, 1] = np.asarray(dt_bias)[c]
        gn = np.ascontiguousarray(np.broadcast_to(np.asarray(out_norm, np.float32)[None, :], (64, 128)))
        pw = np.ascontiguousarray(np.asarray(pool_w)[g][:, hf * 128:(hf + 1) * 128])
        psc = np.ascontiguousarray(np.asarray(pool_scale)[g * 256 + hf * 128: g * 256 + (hf + 1) * 128].reshape(128, 1))
        cm = np.zeros((128, 4, 16), np.float32)
        win = POOL_WINDOWS[g]
        cm[:, g, :] = 1.0 / np.minimum(np.arange(16) + 1, win).astype(np.float32)[None, :]
        in_maps.append(dict(hT=hT, w=w, cw=cw, par=par, gn=gn, pw=pw, psc=psc, cmat=np.ascontiguousarray(cm.reshape(128, 64)), cst=cst))
    res = run_bass_kernel_spmd(nc, in_maps, core_ids=list(range(NCORES)))
    o = np.concatenate([r["mixT"][0:128] for r in res.results], axis=0)
    p = np.concatenate([r["mixT"][128:256] for r in res.results], axis=0)
    return np.concatenate([o, p], axis=0)


def kernel(x, positions, ffn1_norm, ffn1_w_gate, ffn1_w_up, ffn1_w_down, mix_norm,
           ffn2_norm, ffn2_w_gate, ffn2_w_up, ffn2_w_down,
           hyb_w_in, gdn_conv, gdn_a_log, gdn_dt_bias, gdn_out_norm, pool_w, pool_scale, hyb_w_out,
           mla_w_in, mla_q_norm, mla_kv_norm, mla_w_q_up, mla_w_kv_up,
           mla_q_head_norm, mla_k_head_norm, mla_w_out):
    A = lambda a: np.asarray(a)
    depth = 4
    x = A(x)
    xT = np.ascontiguousarray(x[0].T.astype(np.float32))

    def f1(l):
        return dict(norm=A(ffn1_norm)[l], wg=A(ffn1_w_gate)[l], wu=A(ffn1_w_up)[l], wd=A(ffn1_w_down)[l])

    def f2(l):
        return dict(norm=A(ffn2_norm)[l], wg=A(ffn2_w_gate)[l], wu=A(ffn2_w_up)[l], wd=A(ffn2_w_down)[l])

    xT, hT = run_chain(xT, None, None, [f1(0)], A(mix_norm)[0])
    for l in range(depth):
        i = l // 2
        if l % 2 == 0:
            mixT = run_gdn(hT, A(hyb_w_in)[i], A(gdn_conv)[i], A(gdn_a_log)[i], A(gdn_dt_bias)[i], A(gdn_out_norm)[i],
                           A(pool_w)[i], A(pool_scale)[i])
            w_out = A(hyb_w_out)[i]
        else:
            mixT = run_mla(hT, A(positions), A(mla_w_in)[i], A(mla_q_norm)[i], A(mla_kv_norm)[i], A(mla_w_q_up)[i], A(mla_w_kv_up)[i],
                           A(mla_q_head_norm)[i], A(mla_k_head_norm)[i])
            w_out = A(mla_w_out)[i]
        if l + 1 < depth:
            xT, hT = run_chain(xT, mixT, w_out, [f2(l), f1(l + 1)], A(mix_norm)[l + 1])
        else:
            xT, hT = run_chain(xT, mixT, w_out, [f2(l)], None)
    return np.ascontiguousarray(xT.T)[None].astype(np.float32)
```

```python
from contextlib import ExitStack
import numpy as np
import ml_dtypes
import concourse.bass as bass
import concourse.mybir as mybir
from concourse.bass_utils import run_bass_kernel_spmd

F32 = mybir.dt.float32
BF16 = mybir.dt.bfloat16
AF = mybir.ActivationFunctionType
ALU = mybir.AluOpType
AX = mybir.AxisListType

NCORES = 8
D = 2048
DFF = 4096
EPS = 1e-6


class Prog:
    ENG = ("tensor", "vector", "scalar", "gpsimd", "sync")

    def __init__(self, nc):
        self.nc = nc
        self.ops = {e: [] for e in self.ENG}
        self.cnt = {}
        self.last_w = {}
        self.readers = {}
        self.seen = {e: {} for e in self.ENG}

    def add(self, eng, fn, reads=(), writes=(), dma=None, dma_inc=16):
        deps = {}

        def need(sig):
            s, v = sig
            if deps.get(s, 0) < v:
                deps[s] = v

        for r in reads:
            if r in self.last_w:
                need(self.last_w[r])
        for w in writes:
            if w in self.last_w:
                need(self.last_w[w])
            for s, v in self.readers.get(w, {}).items():
                need((s, v))
        waits = []
        for s, v in deps.items():
            if eng == "tensor" and s == "E:tensor":
                continue
            if self.seen[eng].get(s, 0) >= v:
                continue
            self.seen[eng][s] = v
            waits.append((s, v))
        if fn is None:
            self.ops[eng].append((None, waits, None))
            return
        if dma is not None:
            s = "D:" + dma
            self.cnt[s] = self.cnt.get(s, 0) + dma_inc
            sig = (s, self.cnt[s])
            inc = (s, dma_inc)
        else:
            s = "E:" + eng
            self.cnt[s] = self.cnt.get(s, 0) + 1
            sig = (s, self.cnt[s])
            inc = (s, 1)
        for w in writes:
            self.last_w[w] = sig
            self.readers[w] = {}
        for r in reads:
            d = self.readers.setdefault(r, {})
            if d.get(sig[0], 0) < sig[1]:
                d[sig[0]] = sig[1]
        self.ops[eng].append((fn, waits, inc))

    def barrier(self):
        for eng in self.ENG:
            waits = []
            for s, v in self.cnt.items():
                if eng == "tensor" and s == "E:tensor":
                    continue
                if self.seen[eng].get(s, 0) >= v:
                    continue
                self.seen[eng][s] = v
                waits.append((s, v))
            self.ops[eng].append((None, waits, None))

    def finish(self, out_res):
        self.add("sync", None, reads=list(out_res))

    def emit(self):
        nc = self.nc
        with ExitStack() as es:
            sems = {}
            for i, s in enumerate(sorted(self.cnt)):
                sems[s] = es.enter_context(nc.semaphore("s%d" % i))
            block = es.enter_context(nc.Block())

            def runner(ename):
                def run(eng):
                    for fn, waits, inc in self.ops[ename]:
                        for s, v in waits:
                            eng.wait_ge(sems[s], v)
                        if fn is not None:
                            ins = fn(eng)
                            ins.then_inc(sems[inc[0]], inc[1])
                return run

            block.tensor(runner("tensor"))
            block.vector(runner("vector"))
            block.scalar(runner("scalar"))
            block.gpsimd(runner("gpsimd"))
            block.sync(runner("sync"))


class Env:
    def __init__(self, nc, P, ps, pfx, aps, mix_loader=None, h_src=None):
        self.nc, self.P, self.ps, self.pfx, self.aps = nc, P, ps, pfx, aps
        self.mix_loader = mix_loader
        self.h_src = h_src


class Ctx:
    def __init__(self, nc, es, pfx=""):
        self.nc = nc
        self.es = es
        self.n = 0
        self.pfx = pfx

    def sb(self, shape, dt, name=None):
        self.n += 1
        return self.es.enter_context(self.nc.sbuf_tensor(self.pfx + (name or ("t%d" % self.n)), list(shape), dt))

    def ps(self, shape, dt, name=None):
        self.n += 1
        return self.es.enter_context(self.nc.psum_tensor(name or ("p%d" % self.n), list(shape), dt))


def build_chain(tok, has_mix, n_ffn, has_h, env=None):
    nc = env.nc if env else bass.Bass("TRN2", target_bir_lowering=False)
    pfx = env.pfx if env else ""

    class _DT:
        def dram_tensor(self, name, shape, dt, kind="Internal"):
            class _H:
                pass
            h = _H()
            if env:
                h.ap = lambda: env.aps[name]
            else:
                t = nc.dram_tensor(name, shape, dt, kind=kind)
                h.ap = t.ap
            return h
    ncd = _DT()
    NB = tok // 512
    DC = D // 128
    FC = DFF // 128
    xT_in = ncd.dram_tensor("xT", [D, tok], F32, kind="ExternalInput").ap()
    xT_out = ncd.dram_tensor("xT_out", [D, tok], F32, kind="ExternalOutput").ap()
    if has_mix:
        mixT = ncd.dram_tensor("mixT", [D, tok], BF16, kind="ExternalInput").ap()
        w_out = ncd.dram_tensor("w_out", [D, D], F32, kind="ExternalInput").ap()
    ffn_w = []
    for i in range(n_ffn):
        ffn_w.append(dict(
            g=ncd.dram_tensor("f%d_norm" % i, [128, DC], F32, kind="ExternalInput").ap(),
            wg=ncd.dram_tensor("f%d_wg" % i, [D, DFF], F32, kind="ExternalInput").ap(),
            wu=ncd.dram_tensor("f%d_wu" % i, [D, DFF], F32, kind="ExternalInput").ap(),
            wd=ncd.dram_tensor("f%d_wd" % i, [DFF, D], F32, kind="ExternalInput").ap(),
        ))
    if has_h:
        h_norm = ncd.dram_tensor("h_norm", [128, DC], F32, kind="ExternalInput").ap()
        hT_out = ncd.dram_tensor("hT_out", [D, tok], BF16, kind="ExternalOutput").ap()

    with ExitStack() as es:
        cx = Ctx(nc, es, pfx)
        P = env.P if env else Prog(nc)
        xb = cx.sb([128, DC, 512], F32, "xb")
        hT = cx.sb([128, DC, 512], BF16, "hT")
        aT = cx.sb([128, FC, 512], BF16, "aT")
        NW = 2
        wbuf = [cx.sb([128, 16 * 1024], BF16, "wbuf%d" % i) for i in range(NW)]
        sq = [cx.sb([128, 512], BF16, "sq%d" % i) for i in range(2)]
        rstd = cx.sb([128, 512], F32, "rstd")
        sg = [cx.sb([128, 512], F32, "sg%d" % i) for i in range(2)]
        ones = cx.sb([128, 128], BF16, "ones")
        gains = cx.sb([128, (n_ffn + 1) * DC], F32, "gains")
        NPS = 6
        ps = env.ps[:NPS] if env else [cx.ps([128, 512], F32, "ps%d" % i) for i in range(NPS)]
        mstage = cx.sb([128, DC, 512], BF16, "mstage") if (env and env.mix_loader) else None
        st = dict(ps=0, w=0, sq=0, sg=0)

        def next_ps():
            i = st["ps"] % NPS
            st["ps"] += 1
            return i

        P.add("gpsimd", lambda e: e.memset(ones[:], 1.0), writes=["ones"])
        for i in range(n_ffn):
            P.add("sync", lambda e, i=i: e.dma_start(out=gains[:, i * DC:(i + 1) * DC], in_=ffn_w[i]["g"]),
                  writes=["gains%d" % i], dma="gains%d" % i)
        if has_h:
            P.add("sync", lambda e: e.dma_start(out=gains[:, n_ffn * DC:(n_ffn + 1) * DC], in_=h_norm),
                  writes=["gains%d" % n_ffn], dma="gains%d" % n_ffn)

        def load_w(src_ap, nk, c0, ncols):
            slot = st["w"] % NW
            st["w"] += 1
            view = wbuf[slot][:, 0:nk * ncols].rearrange("p (k c) -> p k c", k=nk)
            src = src_ap.rearrange("(k p) c -> p k c", p=128)[:, :, c0:c0 + ncols]
            res = "wbuf%d" % slot
            half = nk // 2
            P.add("gpsimd", lambda e: e.dma_start(out=view[:, 0:half, :], in_=src[:, 0:half, :]),
                  writes=[res + "a"], dma=res + "a")
            P.add("gpsimd", lambda e: e.dma_start(out=view[:, half:nk, :], in_=src[:, half:nk, :]),
                  writes=[res + "b"], dma=res + "b")
            return view, [res + "a", res + "b"], half

        def norm_to_h(gi):
            pi = next_ps()
            for dc in range(DC):
                s = st["sq"] % 2
                st["sq"] += 1
                P.add("scalar", lambda e, dc=dc, s=s: e.activation(out=sq[s][:], in_=xb[:, dc, :], func=AF.Square),
                      reads=["xb"], writes=["sq%d" % s])
                P.add("tensor", lambda e, dc=dc, s=s, pi=pi: e.matmul(ps[pi][:], ones[:], sq[s][:], start=(dc == 0), stop=(dc == DC - 1)),
                      reads=["ones", "sq%d" % s], writes=["ps%d" % pi])
            P.add("scalar", lambda e, pi=pi: e.activation(out=rstd[:], in_=ps[pi][:], func=AF.Sqrt, bias=EPS, scale=1.0 / D),
                  reads=["ps%d" % pi], writes=["rstd"])
            P.add("vector", lambda e: e.reciprocal(out=rstd[:], in_=rstd[:]),
                  reads=["rstd"], writes=["rstd"])
            for dc in range(DC):
                P.add("vector", lambda e, dc=dc: e.scalar_tensor_tensor(
                    out=hT[:, dc, :], in0=xb[:, dc, :], scalar=gains[:, gi * DC + dc:gi * DC + dc + 1], in1=rstd[:],
                    op0=ALU.mult, op1=ALU.mult),
                    reads=["xb", "rstd", "gains%d" % gi], writes=["hT"])

        def proj_residual(src, src_res, w_ap, nk, scale):
            for dg in range(4):
                wv, wres, half = load_w(w_ap, nk, dg * 512, 512)
                for j in range(4):
                    dc = dg * 4 + j
                    pi = next_ps()
                    for k in range(nk):
                        P.add("tensor", lambda e, k=k, j=j, pi=pi, wv=wv: e.matmul(
                            ps[pi][:], wv[:, k, j * 128:(j + 1) * 128], src[:, k, :], start=(k == 0), stop=(k == nk - 1)),
                            reads=[wres[0] if k < half else wres[1], src_res], writes=["ps%d" % pi])
                    P.add("vector", lambda e, dc=dc, pi=pi: e.scalar_tensor_tensor(
                        out=xb[:, dc, :], in0=ps[pi][:], scalar=float(scale), in1=xb[:, dc, :], op0=ALU.mult, op1=ALU.add),
                        reads=["ps%d" % pi, "xb"], writes=["xb"])

        def ffn(i):
            norm_to_h(i)
            fw = ffn_w[i]
            for fg in range(DFF // 512):
                wgv, wgres, half = load_w(fw["wg"], DC, fg * 512, 512)
                wuv, wures, _ = load_w(fw["wu"], DC, fg * 512, 512)
                for j in range(4):
                    fc = fg * 4 + j
                    pg = next_ps()
                    for k in range(DC):
                        P.add("tensor", lambda e, k=k, j=j, pg=pg, wgv=wgv: e.matmul(
                            ps[pg][:], wgv[:, k, j * 128:(j + 1) * 128], hT[:, k, :], start=(k == 0), stop=(k == DC - 1)),
                            reads=[wgres[0] if k < half else wgres[1], "hT"], writes=["ps%d" % pg])
                    pu = next_ps()
                    for k in range(DC):
                        P.add("tensor", lambda e, k=k, j=j, pu=pu, wuv=wuv: e.matmul(
                            ps[pu][:], wuv[:, k, j * 128:(j + 1) * 128], hT[:, k, :], start=(k == 0), stop=(k == DC - 1)),
                            reads=[wures[0] if k < half else wures[1], "hT"], writes=["ps%d" % pu])
                    s = st["sg"] % 2
                    st["sg"] += 1
                    P.add("scalar", lambda e, pg=pg, s=s: e.activation(out=sg[s][:], in_=ps[pg][:], func=AF.Silu),
                          reads=["ps%d" % pg], writes=["sg%d" % s])
                    P.add("vector", lambda e, pu=pu, s=s, fc=fc: e.tensor_tensor(out=aT[:, fc, :], in0=ps[pu][:], in1=sg[s][:], op=ALU.mult),
                          reads=["ps%d" % pu, "sg%d" % s], writes=["aT"])
            proj_residual(aT, "aT", fw["wd"], FC, 0.5)

        for b in range(NB):
            tsl = slice(b * 512, (b + 1) * 512)
            P.add("sync", lambda e, tsl=tsl: e.dma_start(out=xb[:], in_=xT_in.rearrange("(c p) t -> p c t", p=128)[:, :, tsl]),
                  writes=["xb"], dma="xb")
            if has_mix:
                mv = aT[:, 0:DC, :]
                if env and env.mix_loader:
                    env.mix_loader(P, b, mv, [hT, mstage], ["hT", "mstage"], cx)
                else:
                    P.add("sync", lambda e, tsl=tsl, mv=mv: e.dma_start(out=mv, in_=mixT.rearrange("(c p) t -> p c t", p=128)[:, :, tsl]),
                          writes=["aT"], dma="aT")
                proj_residual(mv, "aT", w_out, DC, 1.0)
            for i in range(n_ffn):
                ffn(i)
            P.add("sync", lambda e, tsl=tsl: e.dma_start(out=xT_out.rearrange("(c p) t -> p c t", p=128)[:, :, tsl], in_=xb[:]),
                  reads=["xb"], writes=["dram_x%d" % b], dma="xout")
            if has_h:
                norm_to_h(n_ffn)
                P.add("sync", lambda e, tsl=tsl: e.dma_start(out=hT_out.rearrange("(c p) t -> p c t", p=128)[:, :, tsl], in_=hT[:]),
                      reads=["hT"], writes=["dram_h%d" % b], dma="hout")
        outs = ["dram_x%d" % b for b in range(NB)] + (["dram_h%d" % b for b in range(NB)] if has_h else [])
        if env is None:
            P.finish(outs)
            P.emit()
    return nc


def build_chain2(tok, has_mix, n_ffn, has_h, env=None):
    nc = env.nc if env else bass.Bass("TRN2", target_bir_lowering=False)
    pfx = env.pfx if env else ""

    class _DT:
        def dram_tensor(self, name, shape, dt, kind="Internal"):
            class _H:
                pass
            h = _H()
            if env:
                h.ap = lambda: env.aps[name]
            else:
                t = nc.dram_tensor(name, shape, dt, kind=kind)
                h.ap = t.ap
            return h
    ncd = _DT()
    NB = tok // 512
    DC = D // 128
    FC = DFF // 128
    xT_in = ncd.dram_tensor("xT", [D, tok], F32, kind="ExternalInput").ap()
    xT_out = ncd.dram_tensor("xT_out", [D, tok], F32, kind="ExternalOutput").ap()
    if has_mix:
        mixT = ncd.dram_tensor("mixT", [D, tok], BF16, kind="ExternalInput").ap()
        w_out = ncd.dram_tensor("w_out", [D, D], F32, kind="ExternalInput").ap()
    ffn_w = []
    for i in range(n_ffn):
        ffn_w.append(dict(
            g=ncd.dram_tensor("f%d_norm" % i, [128, DC], F32, kind="ExternalInput").ap(),
            wg=ncd.dram_tensor("f%d_wg" % i, [D, DFF], F32, kind="ExternalInput").ap(),
            wu=ncd.dram_tensor("f%d_wu" % i, [D, DFF], F32, kind="ExternalInput").ap(),
            wd=ncd.dram_tensor("f%d_wd" % i, [DFF, D], F32, kind="ExternalInput").ap(),
        ))
    if has_h:
        h_norm = ncd.dram_tensor("h_norm", [128, DC], F32, kind="ExternalInput").ap()
        hT_out = ncd.dram_tensor("hT_out", [D, tok], BF16, kind="ExternalOutput").ap()

    with ExitStack() as es:
        cx = Ctx(nc, es, pfx)
        P = env.P if env else Prog(nc)
        T = tok
        assert NB == 2
        big = cx.sb([128, FC * T], BF16, "big")
        aT = big[:].rearrange("p (f t) -> p f t", f=FC)
        xb = big[:, 0:DC * 1024].bitcast(F32).rearrange("p (c t) -> p c t", c=DC)
        hT = cx.sb([128, DC, T], BF16, "hT")
        NW = 4
        wbuf = [cx.sb([128, 8192], BF16, "wbuf%d" % i) for i in range(NW)]
        sq = [cx.sb([128, 512], BF16, "sq%d" % i) for i in range(2)]
        rstd = cx.sb([128, 512], F32, "rstd")
        sg = [cx.sb([128, 512], F32, "sg%d" % i) for i in range(2)]
        NXC = 4
        xc = [cx.sb([128, 512], F32, "xc%d" % i) for i in range(NXC)]
        ones = cx.sb([128, 128], BF16, "ones")
        gains = cx.sb([128, (n_ffn + 1) * DC], F32, "gains")
        NPS = 8 if env else 8
        ps = env.ps[:NPS] if env else [cx.ps([128, 512], F32, "ps%d" % i) for i in range(NPS)]
        st = dict(ps=0, w=0, sq=0, sg=0, xc=0)
        xstate = {"ap": xT_in}

        def next_ps():
            i = st["ps"] % NPS
            st["ps"] += 1
            return i

        def xres(dc, blk):
            return "xd_%d_%d" % (dc, blk)

        P.add("gpsimd", lambda e: e.memset(ones[:], 1.0), writes=["ones"])
        for i in range(n_ffn):
            e_dma(P, "sync", gains[:, i * DC:(i + 1) * DC], ffn_w[i]["g"], [], ["gains%d" % i], "gains%d" % i)
        if has_h:
            e_dma(P, "sync", gains[:, n_ffn * DC:(n_ffn + 1) * DC], h_norm, [], ["gains%d" % n_ffn], "gains%d" % n_ffn)

        def load_w(src_ap, nk, c0, ncols):
            slot = st["w"] % NW
            st["w"] += 1
            view = wbuf[slot][:, 0:nk * ncols].rearrange("p (k c) -> p k c", k=nk)
            src = src_ap.rearrange("(k p) c -> p k c", p=128)[:, :, c0:c0 + ncols]
            res = "wbuf%d" % slot
            e_dma(P, "gpsimd", view, src, [], [res], res)
            return view, res

        def norm_to_h(gi):
            for blk in range(NB):
                bs = slice(blk * 512, (blk + 1) * 512)
                e_dma(P, "sync", xb, xstate["ap"].rearrange("(c p) t -> p c t", p=128)[:, :, bs], [xres(dc, blk) for dc in range(DC)], ["aT"], "xb")
                pi = next_ps()
                for dc in range(DC):
                    s_ = st["sq"] % 2
                    st["sq"] += 1
                    e_act(P, sq[s_][:], xb[:, dc, :], AF.Square, ["aT"], ["sq%d" % s_])
                    e_mm(P, ps[pi][:], ones[:], sq[s_][:], dc == 0, dc == DC - 1, ["ones", "sq%d" % s_], ["ps%d" % pi])
                e_act(P, rstd[:], ps[pi][:], AF.Sqrt, ["ps%d" % pi], ["rstd"], bias=EPS, scale=1.0 / D)
                P.add("vector", lambda e: e.reciprocal(out=rstd[:], in_=rstd[:]), reads=["rstd"], writes=["rstd"])
                for dc in range(DC):
                    e_stt(P, hT[:, dc, bs], xb[:, dc, :], gains[:, gi * DC + dc:gi * DC + dc + 1], rstd[:], ALU.mult, ALU.mult,
                          ["aT", "rstd", "gains%d" % gi], ["hT"])

        def proj_residual(src, src_res, w_ap, nk, scale):
            items = [(dg, j, blk) for dg in range(8) for j in range(2) for blk in range(NB)]

            def xload(n):
                dg, j, blk = items[n]
                dc = dg * 2 + j
                xi = n % NXC
                e_dma(P, "sync", xc[xi][:], xstate["ap"][dc * 128:(dc + 1) * 128, blk * 512:(blk + 1) * 512], [xres(dc, blk)], ["xc%d" % xi], "xc%d" % xi)

            xload(0)
            wv = wres = None
            for n, (dg, j, blk) in enumerate(items):
                if j == 0 and blk == 0:
                    wv, wres = load_w(w_ap, nk, dg * 256, 256)
                if n + 1 < len(items):
                    xload(n + 1)
                dc = dg * 2 + j
                bs = slice(blk * 512, (blk + 1) * 512)
                pi = next_ps()
                for k in range(nk):
                    e_mm(P, ps[pi][:], wv[:, k, j * 128:(j + 1) * 128], src[:, k, bs], k == 0, k == nk - 1, [wres, src_res], ["ps%d" % pi])
                xi = n % NXC
                e_stt(P, xc[xi][:], ps[pi][:], float(scale), xc[xi][:], ALU.mult, ALU.add, ["ps%d" % pi, "xc%d" % xi], ["xc%d" % xi])
                e_dma(P, "sync", xT_out[dc * 128:(dc + 1) * 128, bs], xc[xi][:], ["xc%d" % xi], [xres(dc, blk)], "xst%d" % xi)
            xstate["ap"] = xT_out

        def ffn(i):
            norm_to_h(i)
            fw = ffn_w[i]
            for fg in range(DFF // 256):
                wgv, wgres = load_w(fw["wg"], DC, fg * 256, 256)
                wuv, wures = load_w(fw["wu"], DC, fg * 256, 256)
                for j in range(2):
                    fc = fg * 2 + j
                    for blk in range(NB):
                        bs = slice(blk * 512, (blk + 1) * 512)
                        pg = next_ps()
                        for k in range(DC):
                            e_mm(P, ps[pg][:], wgv[:, k, j * 128:(j + 1) * 128], hT[:, k, bs], k == 0, k == DC - 1, [wgres, "hT"], ["ps%d" % pg])
                        pu = next_ps()
                        for k in range(DC):
                            e_mm(P, ps[pu][:], wuv[:, k, j * 128:(j + 1) * 128], hT[:, k, bs], k == 0, k == DC - 1, [wures, "hT"], ["ps%d" % pu])
                        s_ = st["sg"] % 2
                        st["sg"] += 1
                        e_act(P, sg[s_][:], ps[pg][:], AF.Silu, ["ps%d" % pg], ["sg%d" % s_])
                        e_tt(P, aT[:, fc, bs], ps[pu][:], sg[s_][:], ALU.mult, ["ps%d" % pu, "sg%d" % s_], ["aT"])
            proj_residual(aT, "aT", fw["wd"], FC, 0.5)

        if has_mix:
            mv = aT[:, 0:DC, :]
            for blk in range(NB):
                bs = slice(blk * 512, (blk + 1) * 512)
                if env and env.mix_loader:
                    env.mix_loader(P, blk, mv[:, :, bs], [hT[:, :, 0:512], hT[:, :, 512:1024]], ["hTa", "hTb"], cx)
                else:
                    e_dma(P, "sync", mv[:, :, bs], mixT.rearrange("(c p) t -> p c t", p=128)[:, :, bs], [], ["aT"], "aT")
            proj_residual(mv, "aT", w_out, DC, 1.0)
        for i in range(n_ffn):
            ffn(i)
        outs = [xres(dc, blk) for dc in range(DC) for blk in range(NB)]
        if has_h:
            norm_to_h(n_ffn)
            e_dma(P, "sync", hT_out.rearrange("(c p) t -> p c t", p=128), hT[:], ["hT"], ["dram_h"], "hout")
            outs.append("dram_h")
        if env is None:
            P.finish(outs)
            P.emit()
        else:
            env.out_res = outs
    return nc


def gain_cols(g):
    return np.ascontiguousarray(np.asarray(g, np.float32).reshape(D // 128, 128).T)


_NC_CACHE = {}


def run_chain(xT, mixT, w_out, ffns, h_norm):
    S = xT.shape[1]
    tok = S // NCORES
    key = ("chain", tok, mixT is not None, len(ffns), h_norm is not None)
    if key not in _NC_CACHE:
        _NC_CACHE[key] = (build_chain2 if tok == 1024 else build_chain)(tok, mixT is not None, len(ffns), h_norm is not None)
    nc = _NC_CACHE[key]
    in_maps = []
    for c in range(NCORES):
        sl = slice(c * tok, (c + 1) * tok)
        m = {"xT": np.ascontiguousarray(xT[:, sl])}
        if mixT is not None:
            m["mixT"] = np.ascontiguousarray(mixT[:, sl])
            m["w_out"] = w_out
        for i, f in enumerate(ffns):
            m["f%d_norm" % i] = gain_cols(f["norm"])
            m["f%d_wg" % i] = f["wg"]
            m["f%d_wu" % i] = f["wu"]
            m["f%d_wd" % i] = f["wd"]
        if h_norm is not None:
            m["h_norm"] = gain_cols(h_norm)
        in_maps.append(m)
    res = run_bass_kernel_spmd(nc, in_maps, core_ids=list(range(NCORES)))
    xo = np.concatenate([r["xT_out"] for r in res.results], axis=1)
    ho = np.concatenate([r["hT_out"] for r in res.results], axis=1) if h_norm is not None else None
    return xo, ho


def emit_sin(P, cx_tiles, x_ap, x_res, out_ap, out_res, shift, tag):
    import math
    ki, kf, y, m = cx_tiles
    TWO_PI = 2.0 * math.pi
    C1 = 6.28125
    C2 = TWO_PI - C1
    r = [tag + "_ki", tag + "_kf", tag + "_y", tag + "_m"]
    P.add("vector", lambda e: e.tensor_scalar(out=ki, in0=x_ap, scalar1=1.0 / TWO_PI, scalar2=shift / TWO_PI,
                                              op0=ALU.mult, op1=ALU.add), reads=[x_res], writes=[r[0]])
    P.add("vector", lambda e: e.tensor_copy(out=kf, in_=ki), reads=[r[0]], writes=[r[1]])
    P.add("vector", lambda e: e.scalar_tensor_tensor(out=y, in0=kf, scalar=-C1, in1=x_ap, op0=ALU.mult, op1=ALU.add),
          reads=[r[1], x_res], writes=[r[2]])
    P.add("vector", lambda e: e.scalar_tensor_tensor(out=y, in0=kf, scalar=-C2, in1=y, op0=ALU.mult, op1=ALU.add),
          reads=[r[1], r[2]], writes=[r[2]])
    if shift != 0.0:
        P.add("vector", lambda e: e.tensor_scalar_add(out=y, in0=y, scalar1=float(shift)), reads=[r[2]], writes=[r[2]])
    P.add("vector", lambda e: e.tensor_single_scalar(out=m, in_=y, scalar=math.pi, op=ALU.is_gt), reads=[r[2]], writes=[r[3]])
    P.add("vector", lambda e: e.scalar_tensor_tensor(out=y, in0=m, scalar=-TWO_PI, in1=y, op0=ALU.mult, op1=ALU.add),
          reads=[r[3], r[2]], writes=[r[2]])
    P.add("vector", lambda e: e.tensor_single_scalar(out=m, in_=y, scalar=-math.pi, op=ALU.is_lt), reads=[r[2]], writes=[r[3]])
    P.add("vector", lambda e: e.scalar_tensor_tensor(out=y, in0=m, scalar=TWO_PI, in1=y, op0=ALU.mult, op1=ALU.add),
          reads=[r[3], r[2]], writes=[r[2]])
    P.add("vector", lambda e: e.tensor_scalar(out=y, in0=y, scalar1=-math.pi, scalar2=math.pi, op0=ALU.max, op1=ALU.min),
          reads=[r[2]], writes=[r[2]])
    P.add("scalar", lambda e: e.activation(out=out_ap, in_=y, func=AF.Sin), reads=[r[2]], writes=[out_res])


class PsPool:
    def __init__(self, names):
        self.names = list(names)
        self.i = 0

    def next(self):
        n = self.names[self.i % len(self.names)]
        self.i += 1
        return n


def emit_pnorm(P, srcs, np_, count, ones_ap, ones_res, sq_tiles, ps_ap, ps_res, rstd_ap, rstd_res, st):
    n = len(srcs)
    for i, (ap, res) in enumerate(srcs):
        s = st["sq"] % len(sq_tiles)
        st["sq"] += 1
        sqt = sq_tiles[s]
        P.add("scalar", lambda e, ap=ap, sqt=sqt: e.activation(out=sqt[0:np_, :], in_=ap, func=AF.Square),
              reads=[res], writes=["sq%d" % s])
        P.add("tensor", lambda e, i=i, sqt=sqt: e.matmul(ps_ap, ones_ap, sqt[0:np_, :], start=(i == 0), stop=(i == n - 1)),
              reads=[ones_res, "sq%d" % s], writes=[ps_res])
    P.add("scalar", lambda e: e.activation(out=rstd_ap, in_=ps_ap, func=AF.Sqrt, bias=EPS, scale=1.0 / count),
          reads=[ps_res], writes=[rstd_res])
    P.add("vector", lambda e: e.reciprocal(out=rstd_ap, in_=rstd_ap), reads=[rstd_res], writes=[rstd_res])


MLA_SCALE = 192 ** -0.5


def build_mla(S, debug=False, env=None):
    nc = env.nc if env else bass.Bass("TRN2", target_bir_lowering=False)
    pfx = env.pfx if env else ""
    nc0 = nc

    class _DT:
        def dram_tensor(self, name, shape, dt, kind="Internal"):
            class _H:
                pass
            h = _H()
            if env and name in env.aps:
                h.ap = lambda: env.aps[name]
            else:
                t = nc0.dram_tensor(pfx + name, shape, dt, kind=kind)
                h.ap = t.ap
            return h
    ncd = _DT()
    I32 = mybir.dt.int32
    NBK = S // 512
    NKB = S // 128
    DC = D // 128
    hT_d = ncd.dram_tensor("hT", [D, S], BF16, kind="ExternalInput").ap()
    w_in_d = ncd.dram_tensor("w_in", [D, 1088], F32, kind="ExternalInput").ap()
    wq_d = ncd.dram_tensor("wq", [512, 384], F32, kind="ExternalInput").ap()
    wkv_d = ncd.dram_tensor("wkv", [512, 512], F32, kind="ExternalInput").ap()
    latg_d = ncd.dram_tensor("lat_g", [128, 8], F32, kind="ExternalInput").ap()
    hg_d = ncd.dram_tensor("hg", [128, 6], F32, kind="ExternalInput").ap()
    pos_d = ncd.dram_tensor("pos", [1, S], I32, kind="ExternalInput").ap()
    invf_d = ncd.dram_tensor("invf", [32, 1], F32, kind="ExternalInput").ap()
    mask_d = ncd.dram_tensor("mask", [128, 4 * 512], BF16, kind="ExternalInput").ap()
    out_d = ncd.dram_tensor("mixT", [256, S], BF16, kind="ExternalOutput").ap()
    kw = dict(kind="ExternalOutput") if debug else {}
    qn_d = [ncd.dram_tensor("qn_s%d" % h, [128, S], BF16, **kw).ap() for h in range(2)]
    qr_d = [ncd.dram_tensor("qr_s%d" % h, [64, S], BF16, **kw).ap() for h in range(2)]
    kn_d = [ncd.dram_tensor("kn_s%d" % h, [128, S], BF16, **kw).ap() for h in range(2)]
    kr_d = ncd.dram_tensor("kr_s", [64, S], BF16, **kw).ap()
    v_d = [ncd.dram_tensor("v_s%d" % h, [S, 128], BF16, **kw).ap() for h in range(2)]

    with ExitStack() as es:
        cx = Ctx(nc, es, pfx)
        P = env.P if env else Prog(nc)
        st = dict(sq=0)
        ones = cx.sb([128, 128], BF16, "ones")
        lat_g = cx.sb([128, 8], F32, "lat_g_sb")
        hg = cx.sb([128, 6], F32, "hg_sb")
        invs = cx.sb([32, 1], F32, "invs")
        maskt = cx.sb([128, 4, 512], BF16, "maskt")
        ps = env.ps if env else [cx.ps([128, 512], F32, "ps%d" % i) for i in range(8)]
        psd = {"ps%d" % i: ps[i] for i in range(8)}
        P.add("gpsimd", lambda e: e.memset(ones[:], 1.0), writes=["ones"])
        P.add("sync", lambda e: e.dma_start(out=lat_g[:], in_=latg_d), writes=["lat_g"], dma="lat_g")
        P.add("sync", lambda e: e.dma_start(out=hg[:], in_=hg_d), writes=["hg"], dma="hg")
        P.add("sync", lambda e: e.dma_start(out=invs[:], in_=invf_d), writes=["invs"], dma="invs")
        P.add("sync", lambda e: e.dma_start(out=maskt[:], in_=mask_d.rearrange("p (o q) -> p o q", o=4)), writes=["maskt"], dma="maskt")

        with ExitStack() as esA:
            ca = Ctx(nc, esA, pfx)
            w_in = ca.sb([128, DC, 1088], BF16, "w_in_sb")
            wq = ca.sb([128, 4, 384], BF16, "wq_sb")
            wkv = ca.sb([128, 4, 512], BF16, "wkv_sb")
            hTb = [ca.sb([128, DC, 512], BF16, "hTb%d" % i) for i in range(2)]
            lat = ca.sb([128, 8, 512], F32, "lat")
            lnb = [ca.sb([128, 4, 512], BF16, "lnb%d" % i) for i in range(2)]
            sq = [ca.sb([128, 512], BF16, "sqa%d" % i) for i in range(2)]
            rstd = ca.sb([128, 512], F32, "rstd")
            rs32 = ca.sb([32, 512], F32, "rs32")
            pe = [ca.sb([32, 512], F32, "pe%d" % i) for i in range(2)]
            pn = [ca.sb([32, 512], F32, "pn%d" % i) for i in range(2)]
            tt = [ca.sb([32, 512], F32, "tt%d" % i) for i in range(2)]
            rot = [ca.sb([32, 512], BF16, "rot%d" % i) for i in range(2)]
            posi = ca.sb([32, 512], I32, "posi")
            ang = ca.sb([32, 512], F32, "ang")
            ki = ca.sb([32, 512], I32, "ki")
            kf = ca.sb([32, 512], F32, "kf")
            yy = ca.sb([32, 512], F32, "yy")
            mm_ = ca.sb([32, 512], F32, "mm_")
            cs = ca.sb([32, 512], F32, "cs")
            sn = ca.sb([32, 512], F32, "sn")
            nb = [ca.sb([128, 512], BF16, "nb%d" % i) for i in range(2)]
            vt = ca.sb([128, 4, 128], BF16, "vt")
            pp = PsPool(["ps%d" % i for i in range(8)])
            for k in range(4):
                P.add("gpsimd", lambda e, k=k: e.dma_start(out=w_in[:, 4 * k:4 * k + 4, :],
                                                           in_=w_in_d.rearrange("(c p) n -> p c n", p=128)[:, 4 * k:4 * k + 4, :]),
                      writes=["w_in%d" % k], dma="w_in%d" % k)
            P.add("gpsimd", lambda e: e.dma_start(out=wq[:], in_=wq_d.rearrange("(c p) n -> p c n", p=128)), writes=["wq"], dma="wq")
            P.add("gpsimd", lambda e: e.dma_start(out=wkv[:], in_=wkv_d.rearrange("(c p) n -> p c n", p=128)), writes=["wkv"], dma="wkv")
            w_in_res = ["w_in%d" % k for k in range(4)]

            def rope(a_res, gcol0, dst_d, tsl, rs_res):
                for i in range(2):
                    P.add("vector", lambda e, i=i: e.scalar_tensor_tensor(out=pn[i][:], in0=pe[i][:], scalar=hg[0:32, gcol0 + i:gcol0 + i + 1],
                                                                       in1=rs32[:], op0=ALU.mult, op1=ALU.mult),
                          reads=["pe%d" % i, "hg", rs_res], writes=["pn%d" % i])
                P.add("vector", lambda e: e.tensor_tensor(out=tt[0][:], in0=pn[0][:], in1=cs[:], op=ALU.mult), reads=["pn0", "cs"], writes=["tt0"])
                P.add("vector", lambda e: e.tensor_tensor(out=tt[1][:], in0=pn[1][:], in1=sn[:], op=ALU.mult), reads=["pn1", "sn"], writes=["tt1"])
                P.add("vector", lambda e: e.tensor_tensor(out=rot[0][:], in0=tt[0][:], in1=tt[1][:], op=ALU.subtract), reads=["tt0", "tt1"], writes=["rot0"])
                P.add("vector", lambda e: e.tensor_tensor(out=tt[0][:], in0=pn[1][:], in1=cs[:], op=ALU.mult), reads=["pn1", "cs", "rot0"], writes=["tt0"])
                P.add("vector", lambda e: e.tensor_tensor(out=tt[1][:], in0=pn[0][:], in1=sn[:], op=ALU.mult), reads=["pn0", "sn", "rot0"], writes=["tt1"])
                P.add("vector", lambda e: e.tensor_tensor(out=rot[1][:], in0=tt[0][:], in1=tt[1][:], op=ALU.add), reads=["tt0", "tt1"], writes=["rot1"])
                for i in range(2):
                    P.add("gpsimd", lambda e, i=i: e.dma_start(out=dst_d[32 * i:32 * i + 32, tsl], in_=rot[i][:]),
                          reads=["rot%d" % i], writes=[a_res], dma="st_rot%d" % i)

            def load_h(b):
                P.add("sync", lambda e, b=b: e.dma_start(out=hTb[b % 2][:], in_=(env.h_src(b) if env else hT_d.rearrange("(c p) t -> p c t", p=128)[:, :, b * 512:(b + 1) * 512])),
                      writes=["hTb%d" % (b % 2)], dma="hTb%d" % (b % 2))

            load_h(0)
            for b in range(NBK):
                tsl = slice(b * 512, (b + 1) * 512)
                if b + 1 < NBK:
                    load_h(b + 1)
                hb = hTb[b % 2]
                hres = "hTb%d" % (b % 2)
                for oc in range(8):
                    pr = pp.next()
                    for dc in range(DC):
                        P.add("tensor", lambda e, oc=oc, dc=dc, pr=pr, hb=hb: e.matmul(psd[pr][:], w_in[:, dc, oc * 128:(oc + 1) * 128], hb[:, dc, :],
                                                                             start=(dc == 0), stop=(dc == DC - 1)),
                              reads=[w_in_res[dc // 4], hres], writes=[pr])
                    if oc % 2 == 0:
                        P.add("scalar", lambda e, oc=oc, pr=pr: e.copy(out=lat[:, oc, :], in_=psd[pr][:]), reads=[pr], writes=["lat%d" % oc])
                    else:
                        P.add("vector", lambda e, oc=oc, pr=pr: e.tensor_copy(out=lat[:, oc, :], in_=psd[pr][:]), reads=[pr], writes=["lat%d" % oc])
                for i in range(2):
                    pr = pp.next()
                    for dc in range(DC):
                        P.add("tensor", lambda e, i=i, dc=dc, pr=pr, hb=hb: e.matmul(psd[pr][0:32, :], w_in[:, dc, 1024 + 32 * i:1056 + 32 * i], hb[:, dc, :],
                                                                            start=(dc == 0), stop=(dc == DC - 1)),
                              reads=[w_in_res[dc // 4], hres], writes=[pr])
                    P.add("vector", lambda e, i=i, pr=pr: e.tensor_copy(out=pe[i][:], in_=psd[pr][0:32, :]), reads=[pr], writes=["pe%d" % i])
                P.add("sync", lambda e, tsl=tsl: e.dma_start(out=posi[:], in_=pos_d[:, tsl].partition_broadcast(32)), writes=["posi"], dma="posi")
                P.add("vector", lambda e: e.tensor_copy(out=ang[:], in_=posi[:]), reads=["posi"], writes=["ang"])
                P.add("vector", lambda e: e.tensor_scalar(out=ang[:], in0=ang[:], scalar1=invs[:, 0:1], scalar2=None, op0=ALU.mult),
                      reads=["ang", "invs"], writes=["ang"])
                emit_sin(P, (ki[:], kf[:], yy[:], mm_[:]), ang[:], "ang", sn[:], "sn", 0.0, "sc")
                emit_sin(P, (ki[:], kf[:], yy[:], mm_[:]), ang[:], "ang", cs[:], "cs", float(np.pi / 2), "sc")
                pr = pp.next()
                emit_pnorm(P, [(pe[0][:], "pe0"), (pe[1][:], "pe1")], 32, 64, ones[0:32, 0:32], "ones", sq, psd[pr][0:32, :], pr, rs32[:], "rs32", st)
                rope("kr_d%d" % b, 4, kr_d, tsl, "rs32")
                for grp in range(2):
                    pr = pp.next()
                    emit_pnorm(P, [(lat[:, 4 * grp + c, :], "lat%d" % (4 * grp + c)) for c in range(4)], 128, 512, ones[:], "ones", sq,
                               psd[pr][:], pr, rstd[:], "rstd", st)
                    for c in range(4):
                        P.add("vector", lambda e, grp=grp, c=c: e.scalar_tensor_tensor(
                            out=lnb[grp][:, c, :], in0=lat[:, 4 * grp + c, :], scalar=lat_g[:, 4 * grp + c:4 * grp + c + 1], in1=rstd[:],
                            op0=ALU.mult, op1=ALU.mult),
                            reads=["lat%d" % (4 * grp + c), "lat_g", "rstd"], writes=["lnb%d" % grp])
                for h in range(2):
                    for which, (wt, wres, c0, dst, gcol) in enumerate(((wq, "wq", h * 192, qn_d[h], 0), (wkv, "wkv", h * 256, kn_d[h], 1))):
                        pr = pp.next()
                        for c in range(4):
                            P.add("tensor", lambda e, c=c, pr=pr, wt=wt, c0=c0, which=which: e.matmul(
                                psd[pr][:], wt[:, c, c0:c0 + 128], lnb[which][:, c, :], start=(c == 0), stop=(c == 3)),
                                reads=[wres, "lnb%d" % which], writes=[pr])
                        pr2 = pp.next()
                        emit_pnorm(P, [(psd[pr][:], pr)], 128, 128, ones[:], "ones", sq, psd[pr2][:], pr2, rstd[:], "rstd", st)
                        nbt = nb[which]
                        P.add("vector", lambda e, pr=pr, nbt=nbt, gcol=gcol: e.scalar_tensor_tensor(
                            out=nbt[:], in0=psd[pr][:], scalar=hg[:, gcol:gcol + 1], in1=rstd[:], op0=ALU.mult, op1=ALU.mult),
                            reads=[pr, "hg", "rstd"], writes=["nb%d" % which])
                        P.add("gpsimd", lambda e, nbt=nbt, dst=dst, tsl=tsl: e.dma_start(out=dst[:, tsl], in_=nbt[:]),
                              reads=["nb%d" % which], writes=["%s%d_d%d" % ("qn" if which == 0 else "kn", h, b)], dma="st_nb%d" % which)
                    for i in range(2):
                        pr = pp.next()
                        for c in range(4):
                            P.add("tensor", lambda e, c=c, pr=pr, i=i, h=h: e.matmul(
                                psd[pr][0:32, :], wq[:, c, h * 192 + 128 + 32 * i:h * 192 + 160 + 32 * i], lnb[0][:, c, :], start=(c == 0), stop=(c == 3)),
                                reads=["wq", "lnb0"], writes=[pr])
                        P.add("vector", lambda e, i=i, pr=pr: e.tensor_copy(out=pe[i][:], in_=psd[pr][0:32, :]), reads=[pr], writes=["pe%d" % i])
                    pr = pp.next()
                    emit_pnorm(P, [(pe[0][:], "pe0"), (pe[1][:], "pe1")], 32, 64, ones[0:32, 0:32], "ones", sq, psd[pr][0:32, :], pr, rs32[:], "rs32", st)
                    rope("qr%d_d%d" % (h, b), 2, qr_d[h], tsl, "rs32")
                    pr = pp.next()
                    for j in range(4):
                        for c in range(4):
                            P.add("tensor", lambda e, c=c, j=j, pr=pr, h=h: e.matmul(
                                psd[pr][:, j * 128:(j + 1) * 128], lnb[1][:, c, j * 128:(j + 1) * 128], wkv[:, c, h * 256 + 128:h * 256 + 256],
                                start=(c == 0), stop=(c == 3)),
                                reads=["wkv", "lnb1"], writes=[pr])
                    P.add("scalar", lambda e, pr=pr: e.copy(out=vt[:].rearrange("p j d -> p (j d)"), in_=psd[pr][:]), reads=[pr], writes=["vt"])
                    P.add("gpsimd", lambda e, h=h, b=b: e.dma_start(out=v_d[h][b * 512:(b + 1) * 512, :].rearrange("(j p) d -> p j d", p=128), in_=vt[:]),
                          reads=["vt"], writes=["v%d_d%d" % (h, b)], dma="st_vt")

        P.barrier()
        with ExitStack() as esB:
            cb = Ctx(nc, esB, pfx)
            knT = cb.sb([128, S], BF16, "knT")
            krT = cb.sb([64, S], BF16, "krT")
            V = cb.sb([128, NKB, 128], BF16, "V")
            qn = [cb.sb([128, 512], BF16, "qn%d" % i) for i in range(2)]
            qr = [cb.sb([64, 512], BF16, "qr%d" % i) for i in range(2)]
            NPT = 3
            pT = [cb.sb([128, 512], BF16, "pT%d" % i) for i in range(NPT)]
            rl = cb.sb([128, 512], F32, "rl")
            oT = [cb.sb([128, 512], BF16, "oT%d" % i) for i in range(2)]
            sp = PsPool(["ps0", "ps1", "ps2", "ps3"])
            all_scr = lambda pfx: [pfx + "_d%d" % b for b in range(NBK)]
            P.add("sync", lambda e: e.dma_start(out=krT[:], in_=kr_d), reads=all_scr("kr"), writes=["krT"], dma="krT")
            cnt = 0
            qi = 0
            for h in range(2):
                P.add("sync", lambda e, h=h: e.dma_start(out=knT[:], in_=kn_d[h]), reads=all_scr("kn%d" % h), writes=["knT"], dma="knT")
                P.add("sync", lambda e, h=h: e.dma_start(out=V[:], in_=v_d[h].rearrange("(n p) d -> p n d", p=128)),
                      reads=all_scr("v%d" % h), writes=["V"], dma="V")
                for Q in range(NBK):
                    tsl = slice(Q * 512, (Q + 1) * 512)
                    qs = qi % 2
                    qi += 1
                    P.add("sync", lambda e, h=h, tsl=tsl, qs=qs: e.dma_start(out=qn[qs][:], in_=qn_d[h][:, tsl]),
                          reads=all_scr("qn%d" % h), writes=["qn%d" % qs], dma="qn%d" % qs)
                    P.add("sync", lambda e, h=h, tsl=tsl, qs=qs: e.dma_start(out=qr[qs][:], in_=qr_d[h][:, tsl]),
                          reads=all_scr("qr%d" % h), writes=["qr%d" % qs], dma="qr%d" % qs)
                    po = "ps%d" % (4 + qs)
                    pl = "ps%d" % (6 + qs)
                    nkb = 4 * Q + 4
                    def emit_qk(kb, qs=qs):
                        ksl = slice(kb * 128, (kb + 1) * 128)
                        pS = sp.next()
                        P.add("tensor", lambda e, ksl=ksl, pS=pS, qs=qs: e.matmul(psd[pS][:], knT[:, ksl], qn[qs][:], start=True, stop=False),
                              reads=["knT", "qn%d" % qs], writes=[pS])
                        P.add("tensor", lambda e, ksl=ksl, pS=pS, qs=qs: e.matmul(psd[pS][:], krT[:, ksl], qr[qs][:], start=False, stop=True),
                              reads=["krT", "qr%d" % qs], writes=[pS])
                        return pS

                    LA = 2
                    pend = {}
                    for kb in range(min(LA, nkb)):
                        pend[kb] = emit_qk(kb)
                    for kb in range(nkb):
                        if kb + LA < nkb:
                            pend[kb + LA] = emit_qk(kb + LA)
                        pS = pend.pop(kb)
                        pt = cnt % NPT
                        cnt += 1
                        P.add("scalar", lambda e, pS=pS, pt=pt: e.activation(out=pT[pt][:], in_=psd[pS][:], func=AF.Exp, scale=MLA_SCALE),
                              reads=[pS], writes=["pT%d" % pt])
                        if kb >= 4 * Q:
                            o = kb - 4 * Q
                            P.add("vector", lambda e, pt=pt, o=o: e.tensor_tensor(out=pT[pt][:], in0=pT[pt][:], in1=maskt[:, o, :], op=ALU.mult),
                                  reads=["pT%d" % pt, "maskt"], writes=["pT%d" % pt])
                        P.add("tensor", lambda e, kb=kb, pt=pt, po=po, nkb=nkb: e.matmul(psd[po][:], V[:, kb, :], pT[pt][:], start=(kb == 0), stop=(kb == nkb - 1)),
                              reads=["V", "pT%d" % pt], writes=[po])
                        P.add("tensor", lambda e, kb=kb, pt=pt, pl=pl, nkb=nkb: e.matmul(psd[pl][:], ones[:], pT[pt][:], start=(kb == 0), stop=(kb == nkb - 1)),
                              reads=["ones", "pT%d" % pt], writes=[pl])
                    P.add("vector", lambda e, pl=pl: e.reciprocal(out=rl[:], in_=psd[pl][:]), reads=[pl], writes=["rl"])
                    P.add("vector", lambda e, po=po, qs=qs: e.tensor_tensor(out=oT[qs][:], in0=psd[po][:], in1=rl[:], op=ALU.mult),
                          reads=[po, "rl"], writes=["oT%d" % qs])
                    P.add("gpsimd", lambda e, h=h, tsl=tsl, qs=qs: e.dma_start(out=out_d[h * 128:(h + 1) * 128, tsl], in_=oT[qs][:]),
                          reads=["oT%d" % qs], writes=["out%d_%d" % (h, Q)], dma="st_o%d" % qs)
            if env is None:
                P.finish(["out%d_%d" % (h, Q) for h in range(2) for Q in range(NBK)])
        if env is None:
            P.emit()
    return nc


def mla_consts():
    invf = (10000.0 ** (-np.arange(0, 64, 2, dtype=np.float32) / 64)).astype(np.float32)[:, None]
    ki = np.arange(128)[:, None]
    qi = np.arange(512)[None, :]
    mask = np.concatenate([(qi >= o * 128 + ki) for o in range(4)], axis=1).astype(np.float32).astype(ml_dtypes.bfloat16)
    return np.ascontiguousarray(invf), np.ascontiguousarray(mask)


def run_mla(hT, positions, w_in, q_norm, kv_norm, w_q_up, w_kv_up, q_head_norm, k_head_norm, debug=False):
    S = hT.shape[1]
    key = ("mla", S, debug)
    if key not in _NC_CACHE:
        _NC_CACHE[key] = build_mla(S, debug)
    nc = _NC_CACHE[key]
    invf, mask = mla_consts()
    lat_g = np.concatenate([np.asarray(q_norm, np.float32).reshape(4, 128).T, np.asarray(kv_norm, np.float32).reshape(4, 128).T], axis=1)
    hg = np.zeros((128, 6), np.float32)
    qh = np.asarray(q_head_norm, np.float32)
    kh = np.asarray(k_head_norm, np.float32)
    hg[:, 0] = qh[0:128]
    hg[:, 1] = kh[0:128]
    hg[0:32, 2] = qh[128:160]
    hg[0:32, 3] = qh[160:192]
    hg[0:32, 4] = kh[128:160]
    hg[0:32, 5] = kh[160:192]
    pos = np.ascontiguousarray(np.asarray(positions, np.int32).reshape(1, S))
    in_maps = []
    for c in range(NCORES):
        wq = np.ascontiguousarray(np.asarray(w_q_up)[:, c * 384:(c + 1) * 384])
        wkv = np.ascontiguousarray(np.asarray(w_kv_up)[:, c * 512:(c + 1) * 512])
        in_maps.append(dict(hT=hT, w_in=np.asarray(w_in), wq=wq, wkv=wkv, lat_g=np.ascontiguousarray(lat_g), hg=hg,
                            pos=pos, invf=invf, mask=mask))
    res = run_bass_kernel_spmd(nc, in_maps, core_ids=list(range(NCORES)))
    if debug:
        return res.results
    return np.concatenate([r["mixT"] for r in res.results], axis=0)


def e_mm(P, out, lhsT, rhs, start, stop, r, w):
    P.add("tensor", lambda e: e.matmul(out, lhsT, rhs, start=start, stop=stop), reads=r, writes=w)


def e_tr(P, out, in_, ident, r, w):
    P.add("tensor", lambda e: e.transpose(out, in_, ident), reads=r, writes=w)


def e_act(P, out, in_, func, r, w, bias=None, scale=None):
    kw = {}
    if bias is not None:
        kw["bias"] = bias
    if scale is not None:
        kw["scale"] = scale
    P.add("scalar", lambda e: e.activation(out=out, in_=in_, func=func, **kw), reads=r, writes=w)


def e_tt(P, out, in0, in1, op, r, w, eng="vector"):
    P.add(eng, lambda e: e.tensor_tensor(out=out, in0=in0, in1=in1, op=op), reads=r, writes=w)


def e_ts(P, out, in0, s1, s2, op0, op1, r, w, eng="vector"):
    if s2 is None:
        P.add(eng, lambda e: e.tensor_scalar(out=out, in0=in0, scalar1=s1, scalar2=None, op0=op0), reads=r, writes=w)
    else:
        P.add(eng, lambda e: e.tensor_scalar(out=out, in0=in0, scalar1=s1, scalar2=s2, op0=op0, op1=op1), reads=r, writes=w)


def e_stt(P, out, in0, scalar, in1, op0, op1, r, w, eng="vector"):
    P.add(eng, lambda e: e.scalar_tensor_tensor(out=out, in0=in0, scalar=scalar, in1=in1, op0=op0, op1=op1), reads=r, writes=w)


def e_cp(P, out, in_, r, w, eng="vector"):
    if eng == "scalar":
        P.add("scalar", lambda e: e.copy(out=out, in_=in_), reads=r, writes=w)
    else:
        P.add(eng, lambda e: e.tensor_copy(out=out, in_=in_), reads=r, writes=w)


def e_dma(P, q, out, in_, r, w, key):
    P.add(q, lambda e: e.dma_start(out=out, in_=in_), reads=r, writes=w, dma=key)


GW = 770


def build_gdn(S, env=None):
    nc = env.nc if env else bass.Bass("TRN2", target_bir_lowering=False)
    pfx = env.pfx if env else ""
    nc0 = nc

    class _DT:
        def dram_tensor(self, name, shape, dt, kind="Internal"):
            class _H:
                pass
            h = _H()
            if env and name in env.aps:
                h.ap = lambda: env.aps[name]
            else:
                t = nc0.dram_tensor(pfx + name, shape, dt, kind=kind)
                h.ap = t.ap
            return h
    ncd = _DT()
    NBK = S // 512
    DC = D // 128
    hT_d = ncd.dram_tensor("hT", [D, S], BF16, kind="ExternalInput").ap()
    w_d = ncd.dram_tensor("w", [D, GW], F32, kind="ExternalInput").ap()
    cw_d = ncd.dram_tensor("cw", [128, 12], F32, kind="ExternalInput").ap()
    par_d = ncd.dram_tensor("par", [64, 2], F32, kind="ExternalInput").ap()
    gn_d = ncd.dram_tensor("gn", [64, 128], F32, kind="ExternalInput").ap()
    pw_d = ncd.dram_tensor("pw", [256, 128], F32, kind="ExternalInput").ap()
    psc_d = ncd.dram_tensor("psc", [128, 1], F32, kind="ExternalInput").ap()
    cmat_d = ncd.dram_tensor("cmat", [128, 64], F32, kind="ExternalInput").ap()
    cst_d = ncd.dram_tensor("cst", [64, 256], F32, kind="ExternalInput").ap()
    out_d = ncd.dram_tensor("mixT", [256, S], BF16, kind="ExternalOutput").ap()

    with ExitStack() as es:
        cx = Ctx(nc, es, pfx)
        P = env.P if env else Prog(nc)
        st = dict(sq=0)
        f = lambda shape, name: cx.sb(shape, F32, name)
        w_sb = cx.sb([128, DC, GW], BF16, "w_sb")
        hTb = [cx.sb([128, DC, 512], BF16, "hTb%d" % i) for i in range(2)]
        cw = f([128, 12], "cw_sb")
        par = f([64, 2], "par_sb")
        gn = f([64, 128], "gn_sb")
        pw = cx.sb([128, 2, 128], BF16, "pw_sb")
        psc = f([128, 1], "psc_sb")
        cmat = f([128, 4, 16], "cmat_sb")
        cst = f([64, 4, 64], "cst_sb")
        ident = f([128, 128], "ident")
        ones_b = cx.sb([128, 128], BF16, "ones_b")
        ones_f = f([64, 128], "ones_f")
        Aexp = f([64, 1], "Aexp")
        cpad = [f([128, 515], "cpad%d" % i) for i in range(3)]
        cv = [f([128, 512], "cv%d" % i) for i in range(3)]
        sl = [f([128, 512], "sl%d" % i) for i in range(3)]
        sq = [cx.sb([128, 512], BF16, "sqg%d" % i) for i in range(2)]
        rstd = f([128, 512], "rstd")
        qT = f([128, 512], "qT")
        kT = f([128, 512], "kT")
        qdT = f([128, 512], "qdT")
        ktm = f([64, 8, 128], "ktm")
        vtm = f([64, 8, 128], "vtm")
        zab = f([64, 8, 130], "zab")
        e1 = f([64, 8], "e1")
        gg = f([64, 8], "gg")
        beta = f([64, 8], "beta")
        gc = f([64, 8], "gc")
        egc = f([64, 8], "egc")
        ekd = f([64, 8], "ekd")
        egl = f([128, 8], "egl")
        bgc = f([64, 8], "bgc")
        diag = f([64, 8, 64], "diag")
        E = f([64, 8, 64], "E")
        NEB = f([64, 8, 64], "NEB")
        PT = [f([64, 8, 64], "PT%d" % i) for i in range(2)]
        Pm = [f([64, 8, 64], "Pm%d" % i) for i in range(2)]
        R = f([64, 8, 64], "R")
        attnT = f([64, 8, 64], "attnT")
        vb = f([64, 8, 128], "vb")
        kbg = f([64, 8, 128], "kbg")
        kd = f([64, 8, 128], "kd")
        u_sb = f([64, 8, 128], "u_sb")
        wT = f([128, 512], "wT")
        Sst = f([128, 128], "Sst")
        vnew = [f([64, 128], "vnew%d" % i) for i in range(2)]
        obuf = f([64, 8, 128], "obuf")
        osq = f([64, 8, 128], "osq")
        oss = f([64, 8], "oss")
        zs = f([64, 8, 128], "zs")
        oT = cx.sb([128, 512], BF16, "oT")
        upad = [f([128, 528], "upad%d" % i) for i in range(2)]
        sA = [f([128, 528], "sA%d" % i) for i in range(2)]
        sB = [f([128, 528], "sB%d" % i) for i in range(2)]
        pacc = [f([128, 528], "pacc%d" % i) for i in range(2)]
        dif = [cx.sb([128, 512], BF16, "dif%d" % i) for i in range(2)]
        yT = cx.sb([128, 512], BF16, "yT")
        ps = env.ps if env else [cx.ps([128, 512], F32, "ps%d" % i) for i in range(8)]
        psd = {"ps%d" % i: ps[i] for i in range(8)}
        pp = PsPool(["ps%d" % i for i in range(8)])

        for k in range(4):
            e_dma(P, "gpsimd", w_sb[:, 4 * k:4 * k + 4, :], w_d.rearrange("(c p) n -> p c n", p=128)[:, 4 * k:4 * k + 4, :], [], ["w_sb%d" % k], "w_sb%d" % k)
        wres = ["w_sb%d" % k for k in range(4)]
        e_dma(P, "gpsimd", pw[:], pw_d.rearrange("(c p) n -> p c n", p=128), [], ["pw"], "pw")
        e_dma(P, "sync", cw[:], cw_d, [], ["cw"], "cw")
        e_dma(P, "sync", par[:], par_d, [], ["par"], "par")
        e_dma(P, "sync", gn[:], gn_d, [], ["gn"], "gn")
        e_dma(P, "sync", psc[:], psc_d, [], ["psc"], "psc")
        e_dma(P, "sync", cmat[:], cmat_d.rearrange("p (w t) -> p w t", w=4), [], ["cmat"], "cmat")
        e_dma(P, "sync", cst[:], cst_d.rearrange("p (w t) -> p w t", w=4), [], ["cst"], "cst")
        P.add("gpsimd", lambda e: e.memset(ones_b[:], 1.0), writes=["ones_b"])
        P.add("gpsimd", lambda e: e.memset(ones_f[:], 1.0), writes=["ones_f"])
        P.add("gpsimd", lambda e: e.memset(ident[:], 0.0), writes=["ident"])
        P.add("gpsimd", lambda e: e.affine_select(out=ident[:], in_=ident[:], pattern=[[-1, 128]], compare_op=ALU.not_equal, fill=1.0,
                                                  base=0, channel_multiplier=1), reads=["ident"], writes=["ident"])
        P.add("gpsimd", lambda e: e.memset(Sst[:], 0.0), writes=["Sst"])
        for i in range(3):
            P.add("gpsimd", lambda e, i=i: e.memset(cpad[i][:, 0:3], 0.0), writes=["cpad%d" % i])
        for i in range(2):
            P.add("gpsimd", lambda e, i=i: e.memset(upad[i][:, 0:16], 0.0), writes=["upad%d" % i])
        e_act(P, Aexp[:], par[:, 0:1], AF.Exp, ["par"], ["Aexp"])
        Ltri = cst[:, 0, :]
        SM = cst[:, 1, :]
        MN = cst[:, 2, :]
        I64 = cst[:, 3, :]

        def bc_c(ap2):
            return ap2.unsqueeze(1).to_broadcast([64, 8, 64])

        def bc_i(col, n):
            return col.unsqueeze(2).to_broadcast([64, 8, n])

        def flat(t):
            return t[:].rearrange("p c i -> p (c i)")

        def rowbcast(col_ap, col_res, m, pr):
            e_tt(P, diag[:], bc_c(I64), bc_i(col_ap, 64), ALU.mult, ["cst", col_res], ["diag"])
            e_mm(P, psd[pr][0:m, :], ones_f[:, 0:m], flat(diag), True, True, ["ones_f", "diag"], [pr])

        def load_h(b):
            e_dma(P, "sync", hTb[b % 2][:], (env.h_src(b) if env else hT_d.rearrange("(c p) t -> p c t", p=128)[:, :, b * 512:(b + 1) * 512]), [], ["hTb%d" % (b % 2)], "hTb%d" % (b % 2))

        load_h(0)
        for b in range(NBK):
            tsl = slice(b * 512, (b + 1) * 512)
            if b + 1 < NBK:
                load_h(b + 1)
            hb = hTb[b % 2]
            hres = "hTb%d" % (b % 2)
            for wh in range(3):
                pr = pp.next()
                for dc in range(DC):
                    e_mm(P, psd[pr][:], w_sb[:, dc, wh * 128:(wh + 1) * 128], hb[:, dc, :], dc == 0, dc == DC - 1, [wres[dc // 4], hres], [pr])
                e_cp(P, cpad[wh][:, 3:515], psd[pr][:], [pr], ["cpad%d" % wh], eng="scalar")
                e_ts(P, cv[wh][:], cpad[wh][:, 3:515], cw[:, wh * 4 + 3:wh * 4 + 4], None, ALU.mult, None, ["cpad%d" % wh, "cw"], ["cv%d" % wh])
                for j in (2, 1, 0):
                    e_stt(P, cv[wh][:], cpad[wh][:, j:j + 512], cw[:, wh * 4 + j:wh * 4 + j + 1], cv[wh][:], ALU.mult, ALU.add,
                          ["cpad%d" % wh, "cw", "cv%d" % wh], ["cv%d" % wh])
                e_cp(P, cpad[wh][:, 0:3], cpad[wh][:, 512:515], ["cpad%d" % wh], ["cpad%d" % wh], eng="gpsimd")
                e_act(P, sl[wh][:], cv[wh][:], AF.Silu, ["cv%d" % wh], ["sl%d" % wh])
            pr = pp.next()
            emit_pnorm(P, [(sl[0][:], "sl0")], 128, 1.0, ones_b[:], "ones_b", sq, psd[pr][:], pr, rstd[:], "rstd", st)
            e_stt(P, qT[:], sl[0][:], float(128 ** -0.5), rstd[:], ALU.mult, ALU.mult, ["sl0", "rstd"], ["qT"])
            pr = pp.next()
            emit_pnorm(P, [(sl[1][:], "sl1")], 128, 1.0, ones_b[:], "ones_b", sq, psd[pr][:], pr, rstd[:], "rstd", st)
            e_tt(P, kT[:], sl[1][:], rstd[:], ALU.mult, ["sl1", "rstd"], ["kT"])
            for (src, sres, dst, dres) in ((kT, "kT", ktm, "ktm"), (sl[2], "sl2", vtm, "vtm")):
                for half in range(2):
                    pr = pp.next()
                    for cc in range(4):
                        c = half * 4 + cc
                        e_tr(P, psd[pr][0:64, cc * 128:(cc + 1) * 128], src[:, c * 64:(c + 1) * 64], ident[:], [sres, "ident"], [pr])
                    e_cp(P, dst[:, half * 4:half * 4 + 4, :].rearrange("p c d -> p (c d)"), psd[pr][0:64, :], [pr], [dres], eng="scalar" if half else "vector")
            for half in range(4):
                pr = pp.next()
                for cc in range(2):
                    c = half * 2 + cc
                    for dc in range(DC):
                        e_mm(P, psd[pr][0:64, cc * 130:(cc + 1) * 130], hb[:, dc, c * 64:(c + 1) * 64], w_sb[:, dc, 384:514], dc == 0, dc == DC - 1,
                             [wres[dc // 4], hres], [pr])
                e_cp(P, zab[:, half * 2:half * 2 + 2, :].rearrange("p c d -> p (c d)"), psd[pr][0:64, 0:260], [pr], ["zab"], eng="scalar" if half % 2 else "vector")
            e_act(P, e1[:], zab[:, :, 128], AF.Exp, ["zab", "par"], ["e1"], bias=par[:, 1:2])
            e_act(P, e1[:], e1[:], AF.Ln, ["e1"], ["e1"], bias=1.0)
            e_ts(P, gg[:], e1[:], Aexp[:, 0:1], -1.0, ALU.mult, ALU.mult, ["e1", "Aexp"], ["gg"])
            e_act(P, beta[:], zab[:, :, 129], AF.Sigmoid, ["zab"], ["beta"])
            e_act(P, zs[:], zab[:, :, 0:128], AF.Silu, ["zab"], ["zs"])
            pr = pp.next()
            e_mm(P, psd[pr][0:64, 0:8], Ltri, gg[:], True, True, ["cst", "gg"], [pr])
            e_cp(P, gc[:], psd[pr][0:64, 0:8], [pr], ["gc"])
            pr = pp.next()
            e_mm(P, psd[pr][:, 0:8], ones_f[:, :], gg[:], True, True, ["ones_f", "gg"], [pr])
            e_act(P, egl[:], psd[pr][:, 0:8], AF.Exp, [pr], ["egl"])
            e_tt(P, ekd[:], psd[pr][0:64, 0:8], gc[:], ALU.subtract, [pr, "gc"], ["ekd"])
            e_act(P, ekd[:], ekd[:], AF.Exp, ["ekd"], ["ekd"])
            e_act(P, egc[:], gc[:], AF.Exp, ["gc"], ["egc"])
            e_tt(P, bgc[:], beta[:], egc[:], ALU.mult, ["beta", "egc"], ["bgc"])
            pr = pp.next()
            rowbcast(gc[:], "gc", 64, pr)
            e_tt(P, E[:], psd[pr][0:64, :].rearrange("p (c i) -> p c i", c=8), bc_i(gc[:], 64), ALU.subtract, [pr, "gc"], ["E"])
            e_tt(P, E[:], E[:], bc_c(MN), ALU.add, ["E", "cst"], ["E"])
            e_act(P, E[:], E[:], AF.Exp, ["E"], ["E"])
            pr = pp.next()
            rowbcast(beta[:], "beta", 64, pr)
            e_tt(P, NEB[:], psd[pr][0:64, :].rearrange("p (c i) -> p c i", c=8), E[:], ALU.mult, [pr, "E"], ["NEB"])
            e_stt(P, NEB[:], NEB[:], -1.0, bc_c(SM), ALU.mult, ALU.mult, ["NEB", "cst"], ["NEB"])
            pr = pp.next()
            rowbcast(egc[:], "egc", 128, pr)
            e_tt(P, qdT[:], qT[:], psd[pr][:], ALU.mult, ["qT", pr], ["qdT"])
            prk = pp.next()
            prq = pp.next()
            for c in range(8):
                cs_ = slice(c * 64, (c + 1) * 64)
                e_mm(P, psd[prk][0:64, cs_], kT[:, cs_], kT[:, cs_], True, True, ["kT"], [prk])
                e_mm(P, psd[prq][0:64, cs_], kT[:, cs_], qT[:, cs_], True, True, ["kT", "qT"], [prq])
            e_tt(P, flat(PT[0]), psd[prk][0:64, :], flat(NEB), ALU.mult, [prk, "NEB"], ["PT0"])
            e_tt(P, flat(attnT), psd[prq][0:64, :], flat(E), ALU.mult, [prq, "E"], ["attnT"])
            pr = pp.next()
            for c in range(8):
                e_tr(P, psd[pr][0:64, c * 64:(c + 1) * 64], PT[0][:, c, :], ident[0:64, 0:64], ["PT0", "ident"], [pr])
            e_cp(P, flat(Pm[0]), psd[pr][0:64, :], [pr], ["Pm0"], eng="scalar")
            e_tt(P, R[:], PT[0][:], bc_c(I64), ALU.add, ["PT0", "cst"], ["R"])
            cur = 0
            for lvl in range(1, 6):
                nxt = 1 - cur
                pa = pp.next()
                for c in range(8):
                    e_mm(P, psd[pa][0:64, c * 64:(c + 1) * 64], PT[cur][:, c, :], Pm[cur][:, c, :], True, True, ["PT%d" % cur, "Pm%d" % cur], [pa])
                e_cp(P, flat(Pm[nxt]), psd[pa][0:64, :], [pa], ["Pm%d" % nxt], eng="scalar")
                if lvl < 5:
                    pb = pp.next()
                    for c in range(8):
                        e_mm(P, psd[pb][0:64, c * 64:(c + 1) * 64], Pm[cur][:, c, :], PT[cur][:, c, :], True, True, ["PT%d" % cur, "Pm%d" % cur], [pb])
                    e_cp(P, flat(PT[nxt]), psd[pb][0:64, :], [pb], ["PT%d" % nxt])
                pc = pp.next()
                for c in range(8):
                    e_mm(P, psd[pc][0:64, c * 64:(c + 1) * 64], Pm[nxt][:, c, :], R[:, c, :], True, True, ["Pm%d" % nxt, "R"], [pc])
                e_tt(P, flat(R), flat(R), psd[pc][0:64, :], ALU.add, ["R", pc], ["R"])
                cur = nxt
            e_tt(P, vb[:], vtm[:], bc_i(beta[:], 128), ALU.mult, ["vtm", "beta"], ["vb"])
            e_tt(P, kbg[:], ktm[:], bc_i(bgc[:], 128), ALU.mult, ["ktm", "bgc"], ["kbg"])
            e_tt(P, kd[:], ktm[:], bc_i(ekd[:], 128), ALU.mult, ["ktm", "ekd"], ["kd"], eng="gpsimd")
            for half in range(2):
                pr = pp.next()
                for cc in range(4):
                    c = half * 4 + cc
                    e_mm(P, psd[pr][0:64, cc * 128:(cc + 1) * 128], R[:, c, :], vb[:, c, :], True, True, ["R", "vb"], [pr])
                e_cp(P, u_sb[:, half * 4:half * 4 + 4, :].rearrange("p c d -> p (c d)"), psd[pr][0:64, :], [pr], ["u_sb"], eng="scalar" if half else "vector")
            pr = pp.next()
            for c in range(8):
                e_mm(P, psd[pr][:, c * 64:(c + 1) * 64], kbg[:, c, :], R[:, c, :], True, True, ["kbg", "R"], [pr])
            e_cp(P, wT[:], psd[pr][:], [pr], ["wT"], eng="scalar")
            for c in range(8):
                cs_ = slice(c * 64, (c + 1) * 64)
                vn = vnew[c % 2]
                vres = "vnew%d" % (c % 2)
                p1 = pp.next()
                e_mm(P, psd[p1][0:64, 0:128], wT[:, cs_], Sst[:], True, True, ["wT", "Sst"], [p1])
                e_tt(P, vn[:], u_sb[:, c, :], psd[p1][0:64, 0:128], ALU.subtract, ["u_sb", p1], [vres])
                p2 = pp.next()
                e_mm(P, psd[p2][0:64, 0:128], qdT[:, cs_], Sst[:], True, False, ["qdT", "Sst"], [p2])
                e_mm(P, psd[p2][0:64, 0:128], attnT[:, c, :], vn[:], False, True, ["attnT", vres], [p2])
                e_cp(P, obuf[:, c, :], psd[p2][0:64, 0:128], [p2], ["obuf"], eng="scalar")
                p3 = pp.next()
                e_mm(P, psd[p3][:, 0:128], kd[:, c, :], vn[:], True, True, ["kd", vres], [p3])
                e_stt(P, Sst[:], Sst[:], egl[:, c:c + 1], psd[p3][:, 0:128], ALU.mult, ALU.add, ["Sst", "egl", p3], ["Sst"])
            e_tt(P, osq[:], obuf[:], obuf[:], ALU.mult, ["obuf"], ["osq"], eng="gpsimd")
            P.add("vector", lambda e: e.tensor_reduce(out=oss[:], in_=osq[:], axis=AX.X, op=ALU.add), reads=["osq"], writes=["oss"])
            e_act(P, oss[:], oss[:], AF.Sqrt, ["oss"], ["oss"], bias=EPS, scale=1.0 / 128)
            P.add("vector", lambda e: e.reciprocal(out=oss[:], in_=oss[:]), reads=["oss"], writes=["oss"])
            e_tt(P, obuf[:], obuf[:], bc_i(oss[:], 128), ALU.mult, ["obuf", "oss"], ["obuf"])
            e_tt(P, obuf[:], obuf[:], gn[:].unsqueeze(1).to_broadcast([64, 8, 128]), ALU.mult, ["obuf", "gn"], ["obuf"])
            e_tt(P, obuf[:], obuf[:], zs[:], ALU.mult, ["obuf", "zs"], ["obuf"])
            pr = pp.next()
            for c in range(8):
                e_tr(P, psd[pr][:, c * 64:(c + 1) * 64], obuf[:, c, :], ident[0:64, 0:64], ["obuf", "ident"], [pr])
            e_cp(P, oT[:], psd[pr][:], [pr], ["oT"], eng="scalar")
            e_dma(P, "gpsimd", out_d[0:128, tsl], oT[:], ["oT"], ["out_o%d" % b], "st_oT")
            for ch in range(2):
                pr = pp.next()
                for dc in range(DC):
                    e_mm(P, psd[pr][:], w_sb[:, dc, 514 + ch * 128:514 + (ch + 1) * 128], hb[:, dc, :], dc == 0, dc == DC - 1, [wres[dc // 4], hres], [pr])
                up = upad[ch]
                ur = "upad%d" % ch
                e_cp(P, up[:, 16:528], psd[pr][:], [pr], [ur], eng="scalar")
                a_, b_ = sA[ch], sB[ch]
                ar, br = "sA%d" % ch, "sB%d" % ch
                pa_, par_ = pacc[ch], "pacc%d" % ch
                eng = "vector"
                lvl_src = [(up, ur, a_, ar, 1), (a_, ar, b_, br, 2), (b_, br, a_, ar, 4), (a_, ar, b_, br, 8)]
                for wi, (s_, sr, d_, dr, sh) in enumerate(lvl_src):
                    lo = 2 * sh - 1
                    e_tt(P, d_[:, lo:528], s_[:, lo:528], s_[:, lo - sh:528 - sh], ALU.add, [sr], [dr], eng=eng)
                    if wi == 0:
                        e_ts(P, pa_[:, 16:528], d_[:, 16:528], cmat[:, wi, 15:16], None, ALU.mult, None, [dr, "cmat"], [par_], eng=eng)
                    else:
                        e_stt(P, pa_[:, 16:528], d_[:, 16:528], cmat[:, wi, 15:16], pa_[:, 16:528], ALU.mult, ALU.add, [dr, "cmat", par_], [par_], eng=eng)
                    if b == 0:
                        tmp = sB[ch][:, 0:16] if (wi % 2 == 0) else sA[ch][:, 0:16]
                        tres = br if (wi % 2 == 0) else ar
                        e_tt(P, tmp, d_[:, 16:32], cmat[:, wi, :], ALU.mult, [dr, "cmat", tres], [tres], eng=eng)
                        if wi == 0:
                            e_cp(P, pa_[:, 0:16], tmp, [tres, par_], [par_], eng=eng)
                        else:
                            e_tt(P, pa_[:, 0:16], pa_[:, 0:16], tmp, ALU.add, [tres, par_], [par_], eng=eng)
                if b == 0:
                    e_cp(P, pa_[:, 16:32], pa_[:, 0:16], [par_], [par_], eng=eng)
                e_tt(P, dif[ch][:], pa_[:, 16:528], up[:, 16:528], ALU.subtract, [par_, ur], ["dif%d" % ch], eng=eng)
                e_cp(P, up[:, 0:16], up[:, 512:528], [ur, ar, br], [ur], eng=eng)
            pr = pp.next()
            for ch in range(2):
                e_mm(P, psd[pr][:], pw[:, ch, :], dif[ch][:], ch == 0, ch == 1, ["pw", "dif%d" % ch], [pr])
            e_ts(P, yT[:], psd[pr][:], psc[:, 0:1], None, ALU.mult, None, [pr, "psc"], ["yT"])
            e_dma(P, "gpsimd", out_d[128:256, tsl], yT[:], ["yT"], ["out_p%d" % b], "st_yT")
        if env is None:
            P.finish(["out_o%d" % b for b in range(NBK)] + ["out_p%d" % b for b in range(NBK)])
            P.emit()
    return nc


POOL_WINDOWS = (2, 4, 8, 16)


def gdn_consts():
    t = np.arange(64)
    Ltri = (t[:, None] <= t[None, :]).astype(np.float32)
    SM = (t[None, :] > t[:, None]).astype(np.float32)
    MN = np.where(t[None, :] >= t[:, None], 0.0, -1e4).astype(np.float32)
    I64 = np.eye(64, dtype=np.float32)
    return np.ascontiguousarray(np.concatenate([Ltri, SM, MN, I64], axis=1))


def run_gdn(hT, w_in, conv_w, a_log, dt_bias, out_norm, pool_w, pool_scale):
    S = hT.shape[1]
    key = ("gdn", S)
    if key not in _NC_CACHE:
        _NC_CACHE[key] = build_gdn(S)
    nc = _NC_CACHE[key]
    w_in = np.asarray(w_in)
    conv_w = np.asarray(conv_w)
    cst = gdn_consts()
    in_maps = []
    for c in range(NCORES):
        g, hf = c // 2, c % 2
        cols = np.concatenate([np.arange(c * 128, (c + 1) * 128), 1024 + np.arange(c * 128, (c + 1) * 128), 2048 + np.arange(c * 128, (c + 1) * 128),
                               3072 + np.arange(c * 128, (c + 1) * 128), [4096 + c], [4104 + c], 4112 + np.arange(g * 256, (g + 1) * 256)])
        w = np.ascontiguousarray(w_in[:, cols])
        cw = np.zeros((128, 12), np.float32)
        for wh in range(3):
            cw[:, wh * 4:(wh + 1) * 4] = conv_w[:, wh * 1024 + c * 128: wh * 1024 + (c + 1) * 128].T
        par = np.zeros((64, 2), np.float32)
        par[:, 0] = np.asarray(a_log)[c]
        par[:, 1] = np.asarray(dt_bias)[c]
        gn = np.ascontiguousarray(np.broadcast_to(np.asarray(out_norm, np.float32)[None, :], (64, 128)))
        pw = np.ascontiguousarray(np.asarray(pool_w)[g][:, hf * 128:(hf + 1) * 128])
        psc = np.ascontiguousarray(np.asarray(pool_scale)[g * 256 + hf * 128: g * 256 + (hf + 1) * 128].reshape(128, 1))
        cm = np.zeros((128, 4, 16), np.float32)
        win = POOL_WINDOWS[g]
        cm[:, g, :] = 1.0 / np.minimum(np.arange(16) + 1, win).astype(np.float32)[None, :]
        in_maps.append(dict(hT=hT, w=w, cw=cw, par=par, gn=gn, pw=pw, psc=psc, cmat=np.ascontiguousarray(cm.reshape(128, 64)), cst=cst))
    res = run_bass_kernel_spmd(nc, in_maps, core_ids=list(range(NCORES)))
    o = np.concatenate([r["mixT"][0:128] for r in res.results], axis=0)
    p = np.concatenate([r["mixT"][128:256] for r in res.results], axis=0)
    return np.concatenate([o, p], axis=0)


def kernel_unfused(x, positions, ffn1_norm, ffn1_w_gate, ffn1_w_up, ffn1_w_down, mix_norm,
           ffn2_norm, ffn2_w_gate, ffn2_w_up, ffn2_w_down,
           hyb_w_in, gdn_conv, gdn_a_log, gdn_dt_bias, gdn_out_norm, pool_w, pool_scale, hyb_w_out,
           mla_w_in, mla_q_norm, mla_kv_norm, mla_w_q_up, mla_w_kv_up,
           mla_q_head_norm, mla_k_head_norm, mla_w_out):
    A = lambda a: np.asarray(a)
    depth = 4
    x = A(x)
    xT = np.ascontiguousarray(x[0].T.astype(np.float32))

    def f1(l):
        return dict(norm=A(ffn1_norm)[l], wg=A(ffn1_w_gate)[l], wu=A(ffn1_w_up)[l], wd=A(ffn1_w_down)[l])

    def f2(l):
        return dict(norm=A(ffn2_norm)[l], wg=A(ffn2_w_gate)[l], wu=A(ffn2_w_up)[l], wd=A(ffn2_w_down)[l])

    xT, hT = run_chain(xT, None, None, [f1(0)], A(mix_norm)[0])
    for l in range(depth):
        i = l // 2
        if l % 2 == 0:
            mixT = run_gdn(hT, A(hyb_w_in)[i], A(gdn_conv)[i], A(gdn_a_log)[i], A(gdn_dt_bias)[i], A(gdn_out_norm)[i],
                           A(pool_w)[i], A(pool_scale)[i])
            w_out = A(hyb_w_out)[i]
        else:
            mixT = run_mla(hT, A(positions), A(mla_w_in)[i], A(mla_q_norm)[i], A(mla_kv_norm)[i], A(mla_w_q_up)[i], A(mla_w_kv_up)[i],
                           A(mla_q_head_norm)[i], A(mla_k_head_norm)[i])
            w_out = A(mla_w_out)[i]
        if l + 1 < depth:
            xT, hT = run_chain(xT, mixT, w_out, [f2(l), f1(l + 1)], A(mix_norm)[l + 1])
        else:
            xT, hT = run_chain(xT, mixT, w_out, [f2(l)], None)
    return np.ascontiguousarray(xT.T)[None].astype(np.float32)


SEQ = 8192
TOK = SEQ // NCORES


def build_fused():
    nc = bass.Bass("TRN2", target_bir_lowering=False, num_devices=NCORES)
    P = Prog(nc)
    I32 = mybir.dt.int32
    DC = D // 128

    def ext(name, shape, dt):
        return nc.dram_tensor(name, list(shape), dt, kind="ExternalInput").ap()

    def internal(name, shape, dt):
        return nc.dram_tensor(name, list(shape), dt).ap()

    x_in = ext("xT", [D, TOK], F32)
    x_out = nc.dram_tensor("xT_out", [D, TOK], F32, kind="ExternalOutput").ap()
    sel_d = ext("sel", [128, 8], F32)
    ffn = []
    for k in range(8):
        ffn.append(dict(g=ext("F%d_norm" % k, [128, DC], F32), wg=ext("F%d_wg" % k, [D, DFF], F32),
                        wu=ext("F%d_wu" % k, [D, DFF], F32), wd=ext("F%d_wd" % k, [DFF, D], F32)))
    hn = [ext("hn%d" % l, [128, DC], F32) for l in range(4)]
    wo = [ext("wo%d" % l, [D, D], F32) for l in range(4)]
    gd = []
    for i in range(2):
        gd.append(dict(w=ext("g%d_w" % i, [D, GW], F32), cw=ext("g%d_cw" % i, [128, 12], F32), par=ext("g%d_par" % i, [64, 2], F32),
                       gn=ext("g%d_gn" % i, [64, 128], F32), pw=ext("g%d_pw" % i, [256, 128], F32), psc=ext("g%d_psc" % i, [128, 1], F32)))
    cmat_d = ext("cmat", [128, 64], F32)
    cst_d = ext("cst", [64, 256], F32)
    ml = []
    for i in range(2):
        ml.append(dict(w_in=ext("m%d_w_in" % i, [D, 1088], F32), wq=ext("m%d_wq" % i, [512, 384], F32), wkv=ext("m%d_wkv" % i, [512, 512], F32),
                       lat_g=ext("m%d_lat_g" % i, [128, 8], F32), hg=ext("m%d_hg" % i, [128, 6], F32)))
    pos_d = ext("pos", [1, SEQ], I32)
    invf_d = ext("invf", [32, 1], F32)
    mask_d = ext("mask", [128, 4 * 512], BF16)
    xs = [internal("xs%d" % i, [D, TOK], F32) for i in range(4)]
    hloc = [internal("hloc%d" % l, [D, TOK], BF16) for l in range(4)]
    hall = [internal("hall%d" % l, [NCORES * D, TOK], BF16) for l in range(4)]
    mloc = [internal("mloc%d" % l, [256, SEQ], BF16) for l in range(4)]
    mall = [internal("mall%d" % l, [NCORES * 256, SEQ], BF16) for l in range(4)]

    with ExitStack() as es0:
        ps = [es0.enter_context(nc.psum_tensor("ps%d" % i, [128, 512], F32)) for i in range(8)]

        def allgather(src, dst, key):
            P.barrier()
            P.add("gpsimd", lambda e: e.collective_compute("AllGather", ALU.bypass, replica_groups=[list(range(NCORES))], ins=[src], outs=[dst]),
                  writes=[key], dma=key, dma_inc=1)
            P.barrier()

        def make_mix_loader(l):
            state = {}

            def loader(P_, b, mv, stages, stage_res, cx):
                if "sel" not in state:
                    state["sel"] = cx.sb([128, 8], F32, "selt")
                    e_dma(P_, "sync", state["sel"][:], sel_d, [], ["selt"], "selt")
                selt = state["sel"]
                for j in range(NCORES):
                    stg, res = stages[j % 2], stage_res[j % 2]
                    src = mall[l][:, j * TOK + b * 512: j * TOK + (b + 1) * 512].rearrange("(k p) t -> p k t", p=128)
                    e_dma(P_, "sync", stg[:], src, [], [res], "mst%d" % (j % 2))
                    if j == 0:
                        e_ts(P_, mv, stg[:], selt[:, 0:1], None, ALU.mult, None, [res, "selt"], ["aT"])
                    else:
                        e_stt(P_, mv, stg[:], selt[:, j:j + 1], mv, ALU.mult, ALU.add, [res, "selt", "aT"], ["aT"])
            return loader

        def make_h_src(l):
            def h_src(b):
                r, cs = b // 2, (b % 2) * 512
                return hall[l][r * D:(r + 1) * D, cs:cs + 512].rearrange("(c p) t -> p c t", p=128)
            return h_src

        last_env = [None]

        def chain_phase(p):
            ffs = [ffn[0]] if p == 0 else ([ffn[2 * p - 1], ffn[2 * p]] if p < 4 else [ffn[7]])
            aps = {"xT": x_in if p == 0 else xs[p - 1], "xT_out": xs[p] if p < 4 else x_out}
            for i, fw in enumerate(ffs):
                aps["f%d_norm" % i] = fw["g"]
                aps["f%d_wg" % i] = fw["wg"]
                aps["f%d_wu" % i] = fw["wu"]
                aps["f%d_wd" % i] = fw["wd"]
            if p > 0:
                aps["mixT"] = mall[p - 1]
                aps["w_out"] = wo[p - 1]
            if p < 4:
                aps["h_norm"] = hn[p]
                aps["hT_out"] = hloc[p]
            env = Env(nc, P, ps, "c%d_" % p, aps, mix_loader=make_mix_loader(p - 1) if p > 0 else None)
            build_chain2(TOK, p > 0, len(ffs), p < 4, env=env)
            last_env[0] = env

        chain_phase(0)
        for l in range(4):
            allgather(hloc[l], hall[l], "cc")
            i = l // 2
            if l % 2 == 0:
                aps = {"hT": hall[l], "w": gd[i]["w"], "cw": gd[i]["cw"], "par": gd[i]["par"], "gn": gd[i]["gn"], "pw": gd[i]["pw"],
                       "psc": gd[i]["psc"], "cmat": cmat_d, "cst": cst_d, "mixT": mloc[l]}
                build_gdn(SEQ, env=Env(nc, P, ps, "g%d_" % l, aps, h_src=make_h_src(l)))
            else:
                aps = {"hT": hall[l], "w_in": ml[i]["w_in"], "wq": ml[i]["wq"], "wkv": ml[i]["wkv"], "lat_g": ml[i]["lat_g"], "hg": ml[i]["hg"],
                       "pos": pos_d, "invf": invf_d, "mask": mask_d, "mixT": mloc[l]}
                build_mla(SEQ, env=Env(nc, P, ps, "m%d_" % l, aps, h_src=make_h_src(l)))
            allgather(mloc[l], mall[l], "cc")
            chain_phase(l + 1)
        P.finish(last_env[0].out_res)
        P.emit()
    return nc


def kernel(x, positions, ffn1_norm, ffn1_w_gate, ffn1_w_up, ffn1_w_down, mix_norm,
           ffn2_norm, ffn2_w_gate, ffn2_w_up, ffn2_w_down,
           hyb_w_in, gdn_conv, gdn_a_log, gdn_dt_bias, gdn_out_norm, pool_w, pool_scale, hyb_w_out,
           mla_w_in, mla_q_norm, mla_kv_norm, mla_w_q_up, mla_w_kv_up,
           mla_q_head_norm, mla_k_head_norm, mla_w_out):
    A = lambda a: np.asarray(a)
    if "fused" not in _NC_CACHE:
        _NC_CACHE["fused"] = build_fused()
    nc = _NC_CACHE["fused"]
    xT = np.ascontiguousarray(A(x)[0].T.astype(np.float32))
    common = {}
    f1 = (A(ffn1_norm), A(ffn1_w_gate), A(ffn1_w_up), A(ffn1_w_down))
    f2 = (A(ffn2_norm), A(ffn2_w_gate), A(ffn2_w_up), A(ffn2_w_down))
    for k in range(8):
        l, src = k // 2, (f1 if k % 2 == 0 else f2)
        common["F%d_norm" % k] = gain_cols(src[0][l])
        common["F%d_wg" % k] = src[1][l]
        common["F%d_wu" % k] = src[2][l]
        common["F%d_wd" % k] = src[3][l]
    for l in range(4):
        common["hn%d" % l] = gain_cols(A(mix_norm)[l])
        i = l // 2
        if l % 2 == 0:
            w = A(hyb_w_out)[i]
            order = np.concatenate([np.arange(h * 1024 + r * 128, h * 1024 + (r + 1) * 128) for r in range(8) for h in range(2)])
            common["wo%d" % l] = np.ascontiguousarray(w[order])
        else:
            common["wo%d" % l] = A(mla_w_out)[i]
    invf, mask = mla_consts()
    common["pos"] = np.ascontiguousarray(A(positions).astype(np.int32).reshape(1, SEQ))
    common["invf"] = invf
    common["mask"] = mask
    common["cst"] = gdn_consts()
    for i in range(2):
        common["m%d_w_in" % i] = A(mla_w_in)[i]
        common["m%d_lat_g" % i] = np.ascontiguousarray(np.concatenate([A(mla_q_norm)[i].astype(np.float32).reshape(4, 128).T,
                                                                    A(mla_kv_norm)[i].astype(np.float32).reshape(4, 128).T], axis=1))
        hg = np.zeros((128, 6), np.float32)
        qh, kh = A(mla_q_head_norm)[i].astype(np.float32), A(mla_k_head_norm)[i].astype(np.float32)
        hg[:, 0], hg[:, 1] = qh[0:128], kh[0:128]
        hg[0:32, 2], hg[0:32, 3], hg[0:32, 4], hg[0:32, 5] = qh[128:160], qh[160:192], kh[128:160], kh[160:192]
        common["m%d_hg" % i] = hg
        common["g%d_gn" % i] = np.ascontiguousarray(np.broadcast_to(A(gdn_out_norm)[i].astype(np.float32)[None, :], (64, 128)))
    in_maps = []
    for c in range(NCORES):
        m = dict(common)
        m["xT"] = np.ascontiguousarray(xT[:, c * TOK:(c + 1) * TOK])
        sel = np.zeros((128, 8), np.float32)
        sel[:, c] = 1.0
        m["sel"] = sel
        g, hf = c // 2, c % 2
        cm = np.zeros((128, 4, 16), np.float32)
        cm[:, g, :] = 1.0 / np.minimum(np.arange(16) + 1, POOL_WINDOWS[g]).astype(np.float32)[None, :]
        m["cmat"] = np.ascontiguousarray(cm.reshape(128, 64))
        for i in range(2):
            w_in = A(hyb_w_in)[i]
            cols = np.concatenate([np.arange(c * 128, (c + 1) * 128), 1024 + np.arange(c * 128, (c + 1) * 128), 2048 + np.arange(c * 128, (c + 1) * 128),
                                   3072 + np.arange(c * 128, (c + 1) * 128), [4096 + c], [4104 + c], 4112 + np.arange(g * 256, (g + 1) * 256)])
            m["g%d_w" % i] = np.ascontiguousarray(w_in[:, cols])
            cw = np.zeros((128, 12), np.float32)
            conv_w = A(gdn_conv)[i]
            for wh in range(3):
                cw[:, wh * 4:(wh + 1) * 4] = conv_w[:, wh * 1024 + c * 128: wh * 1024 + (c + 1) * 128].T
            m["g%d_cw" % i] = cw
            par = np.zeros((64, 2), np.float32)
            par[:, 0] = A(gdn_a_log)[i][c]
            par[:, 1] = A(gdn_dt_bias)[i][c]
            m["g%d_par" % i] = par
            m["g%d_pw" % i] = np.ascontiguousarray(A(pool_w)[i][g][:, hf * 128:(hf + 1) * 128])
            m["g%d_psc" % i] = np.ascontiguousarray(A(pool_scale)[i][g * 256 + hf * 128: g * 256 + (hf + 1) * 128].reshape(128, 1).astype(np.float32))
            m["m%d_wq" % i] = np.ascontiguousarray(A(mla_w_q_up)[i][:, c * 384:(c + 1) * 384])
            m["m%d_wkv" % i] = np.ascontiguousarray(A(mla_w_kv_up)[i][:, c * 512:(c + 1) * 512])
        in_maps.append(m)
    res = run_bass_kernel_spmd(nc, in_maps, core_ids=list(range(NCORES)))
    xo = np.concatenate([r["xT_out"] for r in res.results], axis=1)
    return np.ascontiguousarray(xo.T)[None].astype(np.float32)
```
